# Optimizing a Trainium2 kernel written in Bass

```python
import math
import jax, jax.numpy as jnp
from jax import lax
import numpy as np

D_MODEL = 1024
BATCH = 2
SEQ = 8192
DEPTH = 4

D_MIX = D_MODEL
POOL_WINDOWS = (2, 4, 8, 16)
POOL_GROUPS = len(POOL_WINDOWS)
POOL_CH = 64
POOL_WIDTH = POOL_GROUPS * POOL_CH
MAX_WIN = max(POOL_WINDOWS)
HGRN_WIDTH = D_MIX - POOL_WIDTH
HGRN_HEAD_DIM = 128
HGRN_HEADS = HGRN_WIDTH // HGRN_HEAD_DIM
CHUNK = 64
D_FF = 4 * D_MODEL
RMS_EPS = 1e-5
IN_WIDTH = POOL_WIDTH + 4 * HGRN_WIDTH

kernel_name = "hymba_pool_hgrn2_hybrid"


def rms_norm(x, g, eps=RMS_EPS):
    xf = x.astype(jnp.float32)
    y = xf * lax.rsqrt(jnp.mean(xf * xf, axis=-1, keepdims=True) + eps)
    return (y * g.astype(jnp.float32)).astype(x.dtype)


def pool_mixer(u, w, scale):
    B, S, _ = u.shape
    uf = u.astype(jnp.float32)
    c = jnp.cumsum(uf, axis=1)
    cp = jnp.pad(c, ((0, 0), (MAX_WIN, 0), (0, 0)))
    t = jnp.arange(S)
    groups = []
    for gi, win in enumerate(POOL_WINDOWS):
        lo, hi = gi * POOL_CH, (gi + 1) * POOL_CH
        win_sum = c[:, :, lo:hi] - cp[:, MAX_WIN - win:MAX_WIN - win + S, lo:hi]
        cnt = jnp.minimum(t + 1, win).astype(jnp.float32)[None, :, None]
        groups.append(win_sum / cnt - uf[:, :, lo:hi])
    p = jnp.stack(groups, axis=2)
    y = jnp.einsum('bsgc,gcd->bsgd', p, w.astype(jnp.float32)).reshape(B, S, POOL_WIDTH)
    return (y * scale.astype(jnp.float32)).astype(u.dtype)


def _to_chunks(t, B, S):
    return t.reshape(B, S // CHUNK, CHUNK, HGRN_HEADS, HGRN_HEAD_DIM).transpose(1, 0, 3, 2, 4)


def hgrn2_mixer(q_raw, f_raw, i_raw, g_raw, lb, norm_g):
    B, S, _ = q_raw.shape
    z = f_raw.astype(jnp.float32)
    lbf = lb.astype(jnp.float32)
    log_f = jnp.logaddexp(jnp.log(lbf), jnp.log1p(-lbf) + jax.nn.log_sigmoid(z))
    k = (1.0 - lbf) * jax.nn.sigmoid(-z)
    q = jax.nn.silu(q_raw.astype(jnp.float32))
    v = i_raw.astype(jnp.float32)
    qc, kc, vc, fc = (_to_chunks(a, B, S) for a in (q, k, v, log_f))
    causal = jnp.tril(jnp.ones((CHUNK, CHUNK), dtype=bool))[None, None, :, :, None]

    def chunk_step(state, inp):
        qb, kb, vb, lfb = inp
        b = jnp.cumsum(lfb, axis=2)
        diff = jnp.where(causal, b[:, :, :, None, :] - b[:, :, None, :, :], -jnp.inf)
        scores = jnp.einsum('bhtk,bhsk,bhtsk->bhts', qb, kb, jnp.exp(diff))
        o = (jnp.einsum('bhts,bhsv->bhtv', scores, vb)
             + jnp.einsum('bhtk,bhkv->bhtv', qb * jnp.exp(b), state))
        b_last = b[:, :, -1:, :]
        new_state = (jnp.exp(b_last[:, :, 0, :])[..., None] * state
                     + jnp.einsum('bhsk,bhsv->bhkv', kb * jnp.exp(b_last - b), vb))
        return new_state, o

    s0 = jnp.zeros((B, HGRN_HEADS, HGRN_HEAD_DIM, HGRN_HEAD_DIM), jnp.float32)
    _, o = lax.scan(chunk_step, s0, (qc, kc, vc, fc))
    o = o.transpose(1, 0, 3, 2, 4).reshape(B, S, HGRN_HEADS, HGRN_HEAD_DIM)
    o = o * lax.rsqrt(jnp.mean(o * o, axis=-1, keepdims=True) + RMS_EPS)
    o = o.reshape(B, S, HGRN_WIDTH) * norm_g.astype(jnp.float32)
    o = o * jax.nn.silu(g_raw.astype(jnp.float32))
    return o.astype(q_raw.dtype)


def setup_inputs(seed: int = 0) -> dict:
    key = jax.random.key(seed)
    ks = jax.random.split(key, 12)
    f32 = jnp.float32
    x = jax.random.normal(ks[0], (BATCH, SEQ, D_MODEL), f32)
    norm_mix_g = 1.0 + 0.05 * jax.random.normal(ks[1], (DEPTH, D_MODEL), f32)
    w_in = jax.random.normal(ks[2], (DEPTH, D_MODEL, IN_WIDTH), f32) * D_MODEL ** -0.5
    pool_w = jax.random.normal(ks[3], (DEPTH, POOL_GROUPS, POOL_CH, POOL_CH), f32) * POOL_CH ** -0.5
    pool_scale = 1.0 + 0.1 * jax.random.normal(ks[4], (DEPTH, POOL_WIDTH), f32)
    hgrn_lb_logits = 0.5 * jax.random.normal(ks[5], (DEPTH, HGRN_WIDTH), f32)
    hgrn_norm_g = 1.0 + 0.05 * jax.random.normal(ks[6], (DEPTH, HGRN_WIDTH), f32)
    w_out = jax.random.normal(ks[7], (DEPTH, D_MIX, D_MODEL), f32) * D_MIX ** -0.5
    norm_mlp_g = 1.0 + 0.05 * jax.random.normal(ks[8], (DEPTH, D_MODEL), f32)
    w_up = jax.random.normal(ks[9], (DEPTH, D_MODEL, D_FF), f32) * D_MODEL ** -0.5
    w_down = jax.random.normal(ks[10], (DEPTH, D_FF, D_MODEL), f32) * D_FF ** -0.5
    final_norm_g = 1.0 + 0.05 * jax.random.normal(ks[11], (D_MODEL,), f32)
    return {"x": x, "norm_mix_g": norm_mix_g, "w_in": w_in, "pool_w": pool_w,
            "pool_scale": pool_scale, "hgrn_lb_logits": hgrn_lb_logits,
            "hgrn_norm_g": hgrn_norm_g, "w_out": w_out, "norm_mlp_g": norm_mlp_g,
            "w_up": w_up, "w_down": w_down, "final_norm_g": final_norm_g}


def reference(x, norm_mix_g, w_in, pool_w, pool_scale, hgrn_lb_logits, hgrn_norm_g,
              w_out, norm_mlp_g, w_up, w_down, final_norm_g):
    lb_cum = jnp.cumsum(jax.nn.softmax(hgrn_lb_logits.astype(jnp.float32), axis=0), axis=0)
    lower_bounds = lb_cum - lb_cum[0:1]
    splits = [POOL_WIDTH + j * HGRN_WIDTH for j in range(4)]
    for l in range(DEPTH):
        h = rms_norm(x, norm_mix_g[l])
        u = h @ w_in[l]
        u_pool, q_raw, f_raw, i_raw, g_raw = jnp.split(u, splits, axis=-1)
        y_pool = pool_mixer(u_pool, pool_w[l], pool_scale[l])
        y_hgrn = hgrn2_mixer(q_raw, f_raw, i_raw, g_raw, lower_bounds[l], hgrn_norm_g[l])
        x = x + jnp.concatenate([y_pool, y_hgrn], axis=-1) @ w_out[l]
        h2 = rms_norm(x, norm_mlp_g[l])
        x = x + jnp.square(jax.nn.relu(h2 @ w_up[l])) @ w_down[l]
    return rms_norm(x, final_norm_g)
```

```python
import numpy as np
import concourse.bass as bass
import concourse.mybir as mybir
from concourse.bass_utils import run_bass_kernel_spmd

F32 = mybir.dt.float32
BF16 = mybir.dt.bfloat16
I32 = mybir.dt.int32
AF = mybir.ActivationFunctionType
ALU = mybir.AluOpType

NCORES = 8
D = 1024
SEQ = 8192
BATCH = 2
DEPTH = 4
TOK = 2048
NT = TOK // 128
NH = 6
DFF = 4096
EPS = 1e-5
MID = 63
WINS = (2, 4, 8, 16)


class Buf:
    def __init__(self, name):
        self.name = name
        self.last_write = None
        self.reads = {}
        self.dsem = None
        self.dcount = 0


class Prog:
    ENGS = ("tensor", "vector", "scalar", "gpsimd", "sync")

    def __init__(self, nc, stack):
        self.nc = nc
        self.stack = stack
        self.streams = {e: [] for e in self.ENGS}
        self.count = {e: 0 for e in self.ENGS}
        self.sem = {}
        for e in ("tensor", "vector", "scalar", "gpsimd"):
            self.sem[e] = stack.enter_context(nc.semaphore("s_" + e))
        self.seen = {e: {} for e in self.ENGS}
        self.nbuf = 0

    def buf(self, name):
        return Buf(name)

    def _dsem(self, b):
        if b.dsem is None:
            self.nbuf += 1
            b.dsem = self.stack.enter_context(self.nc.semaphore("d_%d" % self.nbuf))
        return b.dsem

    def _deps(self, eng, reads, writes):
        deps = []
        for b in reads:
            if b.last_write is not None:
                deps.append(b.last_write)
        for b in writes:
            if b.last_write is not None:
                deps.append(b.last_write)
            deps.extend(b.reads.values())
        out = []
        for d in deps:
            if d[0] == 'e':
                _, e2, idx = d
                if e2 == eng and eng == "tensor":
                    continue
                key = ('e', e2)
                sem = self.sem[e2]
            else:
                _, b2, idx = d
                key = ('d', id(b2))
                sem = b2.dsem
            if self.seen[eng].get(key, 0) >= idx:
                continue
            self.seen[eng][key] = idx
            out.append((sem, idx))
        return out

    def op(self, eng, fn, reads=(), writes=()):
        waits = self._deps(eng, reads, writes)
        sem = self.sem[eng]

        def emit(h, fn=fn, waits=waits, sem=sem):
            for s, v in waits:
                h.wait_ge(s, v)
            fn(h).then_inc(sem, 1)
        self.streams[eng].append(emit)
        self.count[eng] += 1
        idx = self.count[eng]
        for b in reads:
            b.reads[('e', eng)] = ('e', eng, idx)
        for b in writes:
            b.last_write = ('e', eng, idx)
            b.reads = {}

    def dma(self, q, fn, reads=(), writes=(), sembuf=None):
        waits = self._deps(q, reads, writes)
        sb = sembuf if sembuf is not None else (writes[0] if writes else reads[0])
        sem = self._dsem(sb)
        def emit(h, fn=fn, waits=waits, sem=sem):
            for s, v in waits:
                h.wait_ge(s, v)
            fn(h).then_inc(sem, 16)
        self.streams[q].append(emit)
        sb.dcount += 16
        dep = ('d', sb, sb.dcount)
        for b in reads:
            b.reads[('d', id(sb))] = dep
        for b in writes:
            b.last_write = dep
            b.reads = {}

    def alias(self, new_bufs, old_bufs):
        for nb in new_bufs:
            for ob in old_bufs:
                deps = list(ob.reads.items())
                if ob.last_write is not None:
                    d = ob.last_write
                    deps.append(((d[0], d[1] if d[0] == 'e' else id(d[1])), d))
                for k, d in deps:
                    cur = nb.reads.get(k)
                    if cur is None or cur[2] < d[2]:
                        nb.reads[k] = d

    def finish(self, q, bufs):
        lst = [(b.dsem, b.dcount) for b in bufs if b.dsem is not None]

        def emit(h):
            for s, v in lst:
                h.wait_ge(s, v)
        self.streams[q].append(emit)

    def run(self):
        nc = self.nc
        with nc.Block() as block:
            @block.tensor
            def _(h):
                for f in self.streams["tensor"]:
                    f(h)

            @block.vector
            def _(h):
                for f in self.streams["vector"]:
                    f(h)

            @block.scalar
            def _(h):
                for f in self.streams["scalar"]:
                    f(h)

            @block.gpsimd
            def _(h):
                for f in self.streams["gpsimd"]:
                    f(h)

            @block.sync
            def _(h):
                for f in self.streams["sync"]:
                    f(h)


def build_program(do_B, do_A, do_final):
    from contextlib import ExitStack
    nc = bass.Bass("TRN2", target_bir_lowering=False)
    stack = ExitStack()
    P = Prog(nc, stack)

    def dram_in(name, shape, dt=F32):
        return nc.dram_tensor(name, list(shape), dt, kind="ExternalInput").ap()

    def dram_out(name, shape, dt=F32):
        return nc.dram_tensor(name, list(shape), dt, kind="ExternalOutput").ap()

    def sb(name, shape, dt):
        return stack.enter_context(nc.sbuf_tensor(name, list(shape), dt))

    def ps(name, shape, dt):
        return stack.enter_context(nc.psum_tensor(name, list(shape), dt))

    def v3(ap, a):
        return ap.rearrange("p (a b) -> p a b", a=a)

    x_in = dram_in("x_in", [TOK, D])
    consts = dram_in("consts", [128, 386])
    maskd = dram_in("maskd", [128, 128], I32)
    lbl = dram_in("lbl", [DEPTH * NH * 128])
    if do_B:
        lmB = dram_in("lmB", [128, DEPTH])
        g_mix = dram_in("g_mix", [D])
        g_mlp = dram_in("g_mlp", [D])
        w_heads = dram_in("w_heads", [NH, 128, 8, 512])
        w_pool = dram_in("w_pool", [128, 8, 256])
        pool_w2 = dram_in("pool_w2", [2, 128, 128])
        pool_sc = dram_in("pool_sc", [128, 2])
        pmat = dram_in("pmat", [128, 12, 128])
        halo = dram_in("halo", [128, 256])
        ng_in = dram_in("ng", [NH * 128])
        w_out = dram_in("w_out", [128, 8, D])
        w_up = dram_in("w_up", [4, 128, 8, 1024])
        w_dn = dram_in("w_dn", [4, 128, 8, D])
        sall = dram_in("sall", [NCORES, NH, 128, 129])
        sel = dram_in("sel", [128, NCORES])
    if do_A:
        lmA = dram_in("lmA", [128, DEPTH])
        gA_mix = dram_in("gA_mix", [D])
        wA_fi = dram_in("wA_fi", [NH, 128, 8, 256])
        wA_pool = dram_in("wA_pool", [128, 8, 256])
        sloc_out = dram_out("sloc", [NH, 128, 129])
        halo_out = dram_out("halo_o", [128, 256])
        if do_B:
            x_out = dram_out("x_out", [TOK, D])
    if do_final:
        g_fin = dram_in("g_fin", [D])
        y_out = dram_out("y_out", [TOK, D])

    X = sb("X", [128, NT, D], F32)
    HT = sb("HT", [128, 8, TOK], BF16)
    R1 = sb("R1", [128, 16384], BF16)
    R2 = sb("R2", [128, 6400], F32)
    R1f = R1[:, :].bitcast(F32)
    R2b = R2[:, :].bitcast(BF16)
    YT = v3(R1[:, :], 8)
    WU = v3(R1[:, 0:8192], 8)
    WD = v3(R1[:, 8192:16384], 8)
    LG = v3(R1f[:, 0:3072], 4)
    TS = R1f[:, 3072:3840]
    TN = R1f[:, 3840:4608]
    SALL = R2[:, 0:6192].rearrange("p (r h f) -> p r h f", r=NCORES, h=NH)
    PM = v3(R2[:, 0:1536], 12)
    HALO = R2[:, 1536:1792]
    WO = v3(R2b[:, 0:8192], 8)
    AT = v3(R2b[:, 8192:12288], 8)
    SOUT = v3(R2[:, 0:774], NH)

    bX = [P.buf("X%d" % t) for t in range(NT)]
    bHT = [P.buf("HT%d" % t) for t in range(NT)]
    bYT = [P.buf("YT%d" % t) for t in range(NT)]
    bWU = P.buf("WU"); bWD = P.buf("WD"); bLG = P.buf("LG")
    bSALL = P.buf("SALL"); bPM = P.buf("PM"); bHALO = P.buf("HALO"); bWO = P.buf("WO"); bAT = P.buf("AT")
    bSOUT = P.buf("SOUT")

    CON = sb("CON", [128, 386], F32); bCON = P.buf("CON")
    IDB = sb("IDB", [128, 128], BF16); bIDB = P.buf("IDB")
    MSK = sb("MSK", [128, 128], I32); bMSK = P.buf("MSK")
    ident = CON[:, 0:128]
    TmX = CON[:, 128:258]
    Umat = CON[:, 258:386]
    GB = sb("GB", [128, D], F32); bGB = P.buf("GB")
    SQJ = sb("SQJ", [128, 128], BF16); bSQJ = P.buf("SQJ")
    H0 = sb("H0", [128, D], BF16); bH0 = P.buf("H0")
    SS = sb("SS", [128, 8], F32); bSS = P.buf("SS")
    LM = sb("LM", [128, 2 * DEPTH], F32); bLM = P.buf("LM")
    OMLB = sb("OMLB", [128, NH * 128], F32); bOMLB = P.buf("OMLB")
    OMLA = sb("OMLA", [128, NH * 128], F32); bOMLA = P.buf("OMLA")
    EZ = sb("EZ", [128, 2, 128], F32); bEZ = P.buf("EZ")
    KK = sb("KK", [128, 2, 128], F32); bKK = P.buf("KK")
    LF = sb("LF", [128, 2, 128], F32); bLF = P.buf("LF")
    VV = sb("VV", [128, 2, 128], BF16); bVV = P.buf("VV")
    E2 = sb("E2", [128, 2, 128], F32); bE2 = P.buf("E2")
    ED = sb("ED", [128, 2, 128], F32); bED = P.buf("ED")
    EX = sb("EX", [128, 2, 2], F32); bEX = P.buf("EX")
    KE = sb("KE", [128, 2, 128], BF16); bKE = P.buf("KE")
    KD = sb("KD", [128, 2, 128], BF16); bKD = P.buf("KD")
    SST = [sb("SST%d" % h, [128, 128], F32) for h in range(NH)]
    bSST = [P.buf("SST%d" % h) for h in range(NH)]
    BTOT = sb("BTOT", [128, NH], F32); bBTOT = P.buf("BTOT")
    WH = sb("WH", [128, 8, 512], BF16); bWH = P.buf("WH")
    WP = sb("WP", [128, 8, 256], BF16); bWP = P.buf("WP")
    UP = sb("UP", [128, 2, 256], F32); bUP = [P.buf("UP0"), P.buf("UP1")]

    PS_T = ps("PS_T", [128, 8, 128], BF16); bPS_T = P.buf("PS_T")
    PS_A = ps("PS_A", [128, 2, 512], F32); bPS_A = [P.buf("PS_A0"), P.buf("PS_A1")]
    PS_B = ps("PS_B", [128, 512], F32); bPS_Bq = P.buf("PS_B"); bPS_Bb = bPS_Bq
    PS_C = ps("PS_C", [128, 512], F32); bPS_Cr = P.buf("PS_C"); bPS_Ck = bPS_Cr
    PS_D = ps("PS_D", [128, 512], F32); bPS_Ds = P.buf("PS_D"); bPS_Do = bPS_Ds
    PS_E = ps("PS_E", [128, 512], F32); bPS_E = P.buf("PS_E"); bPS_Ex = bPS_E
    PS_F = ps("PS_F", [128, 512], F32); bPS_F = P.buf("PS_F"); bPS_F2 = bPS_F

    x_v = x_in.rearrange("(t p) d -> p t d", p=128)
    for t in range(NT):
        P.dma("sync", lambda h, t=t: h.dma_start(out=X[:, t, :], in_=x_v[:, t, :]), writes=[bX[t]])
    P.dma("sync", lambda h: h.dma_start(out=CON[:, :], in_=consts[:, :]), writes=[bCON])
    P.dma("sync", lambda h: h.dma_start(out=MSK[:, :], in_=maskd[:, :]), writes=[bMSK])
    P.dma("gpsimd", lambda h: h.dma_start(out=IDB[:, :], in_=consts[:, 0:128]), writes=[bIDB])

    def load_bcast(dst, bdst, src):
        P.dma("sync", lambda h: h.dma_start(out=dst, in_=src.partition_broadcast(128)), writes=[bdst])

    load_bcast(R1f[:, 0:3072], bLG, lbl)
    if do_B:
        P.dma("sync", lambda h: h.dma_start(out=LM[:, 0:DEPTH], in_=lmB[:, :]), writes=[bLM])
    if do_A:
        P.dma("sync", lambda h: h.dma_start(out=LM[:, DEPTH:2 * DEPTH], in_=lmA[:, :]), writes=[bLM])
    P.op("scalar", lambda h: h.activation(out=R1f[:, 0:3072], in_=R1f[:, 0:3072], func=AF.Exp), reads=[bLG], writes=[bLG])
    P.op("vector", lambda h: h.tensor_tensor(out=TS, in0=LG[:, 0, :], in1=LG[:, 1, :], op=ALU.add), reads=[bLG], writes=[bLG])
    P.op("vector", lambda h: h.tensor_tensor(out=TS, in0=TS, in1=LG[:, 2, :], op=ALU.add), reads=[bLG], writes=[bLG])
    P.op("vector", lambda h: h.tensor_tensor(out=TS, in0=TS, in1=LG[:, 3, :], op=ALU.add), reads=[bLG], writes=[bLG])
    P.op("vector", lambda h: h.reciprocal(out=TS, in_=TS), reads=[bLG], writes=[bLG])

    def compute_oml(dst, bdst, off):
        P.op("vector", lambda h: h.tensor_scalar(out=TN, in0=LG[:, 0, :], scalar1=LM[:, off:off + 1], scalar2=0.0,
                                                 op0=ALU.mult, op1=ALU.add), reads=[bLG, bLM], writes=[bLG])
        for j in range(1, DEPTH):
            P.op("vector", lambda h, j=j: h.scalar_tensor_tensor(out=TN, in0=LG[:, j, :], scalar=LM[:, off + j:off + j + 1],
                                                                 in1=TN, op0=ALU.mult, op1=ALU.add),
                 reads=[bLG, bLM], writes=[bLG])
        P.op("vector", lambda h: h.tensor_tensor(out=TN, in0=TN, in1=TS, op=ALU.mult), reads=[bLG], writes=[bLG])
        P.op("vector", lambda h: h.tensor_scalar(out=dst, in0=TN, scalar1=-1.0, scalar2=1.0, op0=ALU.mult, op1=ALU.add),
             reads=[bLG], writes=[bdst])
    if do_B:
        compute_oml(OMLB[:, :], bOMLB, 0)
    if do_A:
        compute_oml(OMLA[:, :], bOMLA, DEPTH)
    P.alias(bYT + [bWU, bWD], [bLG])

    def rms_stats(xt, bx):
        P.op("scalar", lambda h: h.activation(out=H0[:, :], in_=xt, func=AF.Square, accum_out=SS[:, 0:1]),
             reads=[bx], writes=[bH0, bSS])
        P.op("scalar", lambda h: h.activation(out=SS[:, 1:2], in_=SS[:, 0:1], func=AF.Ln, scale=1.0 / D, bias=EPS),
             reads=[bSS], writes=[bSS])
        P.op("scalar", lambda h: h.activation(out=SS[:, 2:3], in_=SS[:, 1:2], func=AF.Exp, scale=-0.5),
             reads=[bSS], writes=[bSS])

    def rmsnorm_to_HT(t):
        xt = X[:, t, :]
        rms_stats(xt, bX[t])
        P.op("vector", lambda h: h.scalar_tensor_tensor(out=H0[:, :], in0=xt, scalar=SS[:, 2:3], in1=GB[:, :],
                                                        op0=ALU.mult, op1=ALU.mult),
             reads=[bX[t], bSS, bGB], writes=[bH0])
        for kc in range(8):
            P.op("tensor", lambda h, kc=kc: h.transpose(out=PS_T[:, kc, :], in_=H0[:, kc * 128:(kc + 1) * 128],
                                                        identity=IDB[:, :]),
                 reads=[bH0, bIDB], writes=[bPS_T])
        P.op("scalar", lambda h: h.activation(out=HT[:, :, t * 128:(t + 1) * 128], in_=PS_T[:, :, :], func=AF.Copy),
             reads=[bPS_T], writes=[bHT[t]])

    def scan_head(hd, full, woff, OML, bOML, extra=None):
        ncol = 384 if full else 256
        for g0 in range(0, NT, 2):
            tiles = (g0, g0 + 1)
            for j, t in enumerate(tiles):
                for kc in range(8):
                    P.op("tensor", lambda h, j=j, t=t, kc=kc: h.matmul(
                        PS_A[:, j, 0:ncol], lhsT=HT[:, kc, t * 128:(t + 1) * 128],
                        rhs=WH[:, kc, woff:woff + ncol], start=(kc == 0), stop=(kc == 7)),
                        reads=[bHT[t], bWH], writes=[bPS_A[j]])
            if full:
                for kc in range(8):
                    P.op("tensor", lambda h, kc=kc, g0=g0: h.matmul(
                        PS_B[:, 0:256], lhsT=WH[:, kc, 0:128], rhs=HT[:, kc, g0 * 128:(g0 + 2) * 128],
                        start=(kc == 0), stop=(kc == 7)),
                        reads=[bHT[g0], bHT[g0 + 1], bWH], writes=[bPS_Bq])
            P.op("scalar", lambda h: h.activation(out=EZ[:, :, :], in_=PS_A[:, :, 0:128], func=AF.Exp),
                 reads=bPS_A, writes=[bEZ])
            P.op("gpsimd", lambda h: h.tensor_scalar(out=EZ[:, :, :], in0=EZ[:, :, :], scalar1=1.0, scalar2=1.0,
                                                     op0=ALU.add, op1=ALU.mult), reads=[bEZ], writes=[bEZ])
            P.op("vector", lambda h: h.reciprocal(out=EZ[:, :, :], in_=EZ[:, :, :]), reads=[bEZ], writes=[bEZ])
            for j in range(2):
                P.op("vector", lambda h, j=j: h.tensor_tensor(out=KK[:, j, :], in0=EZ[:, j, :],
                                                              in1=OML[:, hd * 128:(hd + 1) * 128], op=ALU.mult),
                     reads=[bEZ, bOML], writes=[bKK])
            P.op("scalar", lambda h: h.activation(out=LF[:, :, :], in_=KK[:, :, :], func=AF.Ln, scale=-1.0, bias=1.0),
                 reads=[bKK], writes=[bLF])
            P.op("scalar", lambda h: h.activation(out=VV[:, :, :], in_=PS_A[:, :, 128:256], func=AF.Copy),
                 reads=bPS_A, writes=[bVV])
            if full:
                extra["gate"]()
            for j in range(2):
                P.op("tensor", lambda h, j=j: h.matmul(PS_B[:, 256 + j * 128:256 + (j + 1) * 128], lhsT=LF[:, j, :],
                                                       rhs=TmX[:, 0:128], start=True, stop=True),
                     reads=[bLF, bCON], writes=[bPS_Bb])
                P.op("tensor", lambda h, j=j: h.matmul(PS_E[:, 256 + j * 2:256 + j * 2 + 2], lhsT=LF[:, j, :],
                                                       rhs=TmX[:, 128:130], start=True, stop=True),
                     reads=[bLF, bCON], writes=[bPS_Ex])
                P.op("tensor", lambda h, j=j: h.matmul(PS_C[:, j * 128:(j + 1) * 128], lhsT=Umat, rhs=LF[:, j, :],
                                                       start=True, stop=True),
                     reads=[bLF, bCON], writes=[bPS_Cr])
                if full:
                    P.op("tensor", lambda h, j=j: h.matmul(PS_C[:, 256 + j * 128:256 + (j + 1) * 128], lhsT=KK[:, j, :],
                                                           rhs=ident, start=True, stop=True),
                         reads=[bKK, bCON], writes=[bPS_Ck])
            P.op("scalar", lambda h: h.activation(out=ED[:, :, :], in_=v3(PS_C[:, 0:256], 2), func=AF.Exp),
                 reads=[bPS_Cr], writes=[bED])
            P.op("scalar", lambda h: h.activation(out=EX[:, :, :], in_=v3(PS_E[:, 256:260], 2), func=AF.Exp),
                 reads=[bPS_Ex], writes=[bEX])
            if not full:
                for j in range(2):
                    P.op("vector", lambda h, j=j: h.tensor_tensor(out=BTOT[:, hd:hd + 1], in0=BTOT[:, hd:hd + 1],
                                                                  in1=PS_E[:, 256 + 2 * j + 1:256 + 2 * j + 2], op=ALU.add),
                         reads=[bPS_Ex, bBTOT], writes=[bBTOT])
            P.op("vector", lambda h: h.tensor_tensor(out=KD[:, :, :], in0=KK[:, :, :], in1=ED[:, :, :], op=ALU.mult),
                 reads=[bKK, bED], writes=[bKD])
            if full:
                P.op("scalar", lambda h: h.activation(out=E2[:, :, :], in_=v3(PS_B[:, 256:512], 2), func=AF.Exp, scale=-1.0),
                     reads=[bPS_Bb], writes=[bE2])
                P.op("vector", lambda h: h.tensor_tensor(out=KE[:, :, :], in0=v3(PS_C[:, 256:512], 2),
                                                         in1=E2[:, :, :], op=ALU.mult),
                     reads=[bPS_Ck, bE2], writes=[bKE])
                extra["qe"]()
            for j, t in enumerate(tiles):
                if full:
                    extra["out"](j, t)
                P.op("tensor", lambda h, j=j: h.matmul(PS_E[:, 0:128], lhsT=KD[:, j, :], rhs=VV[:, j, :],
                                                       start=True, stop=True),
                     reads=[bKD, bVV], writes=[bPS_E])
                P.op("vector", lambda h, j=j: h.scalar_tensor_tensor(out=SST[hd][:, :], in0=SST[hd][:, :],
                                                                     scalar=EX[:, j, 1:2], in1=PS_E[:, 0:128],
                                                                     op0=ALU.mult, op1=ALU.add),
                     reads=[bSST[hd], bEX, bPS_E], writes=[bSST[hd]])

    store_bufs = []

    if do_B:
        SEL = sb("SEL", [128, NCORES], F32); bSEL = P.buf("SEL")
        TCH = sb("TCH", [128, 2, 128], F32); bTCH = P.buf("TCH")
        NGB = sb("NGB", [128, 128], F32); bNGB = P.buf("NGB")
        EG = sb("EG", [128, 2, 128], F32); bEG = P.buf("EG")
        GG = sb("GG", [128, 2, 128], F32); bGG = P.buf("GG")
        EQ = sb("EQ", [128, 2, 128], F32); bEQ = P.buf("EQ")
        QE = sb("QE", [128, 2, 128], BF16); bQE = P.buf("QE")
        SCM = sb("SCM", [128, 2, 128], BF16); bSCM = P.buf("SCM")
        S0B = sb("S0B", [128, 128], BF16); bS0B = P.buf("S0B")
        YY = sb("YY", [128, 128], BF16); bYY = P.buf("YY")
        PW2 = sb("PW2", [128, 2, 128], BF16); bPW2 = P.buf("PW2")
        PSC = sb("PSC", [128, 2], F32); bPSC = P.buf("PSC")
        PTB = sb("PTB", [128, 128], BF16); bPTB = P.buf("PTB")
        RL = sb("RL", [128, 2, 512], F32); bRL = [P.buf("RL0"), P.buf("RL1")]

        load_bcast(GB[:, :], bGB, g_mix)
        for r in range(NCORES):
            P.dma("sync", lambda h, r=r: h.dma_start(out=SALL[:, r, :, :], in_=sall[r].rearrange("h p f -> p h f")), writes=[bSALL])
        P.dma("sync", lambda h: h.dma_start(out=SEL[:, :], in_=sel[:, :]), writes=[bSEL])
        P.dma("sync", lambda h: h.dma_start(out=PSC[:, :], in_=pool_sc[:, :]), writes=[bPSC])
        P.dma("gpsimd", lambda h: h.dma_start(out=PW2[:, :, :], in_=pool_w2.rearrange("c p d -> p c d")), writes=[bPW2])
        P.dma("gpsimd", lambda h: h.dma_start(out=WP[:, :, :], in_=w_pool[:, :, :]), writes=[bWP])
        P.op("gpsimd", lambda h: h.memset(SCM[:, :, :], 0.0), writes=[bSCM])

        for hd in range(NH if _STAGE >= 1 else 0):
            first = True
            for base in (0, 4):
                Tcur = None
                for r in range(base, base + 3):
                    if r == base:
                        Tcur = SALL[:, r, hd, 0:128]
                    else:
                        dst = TCH[:, r % 2, :]
                        P.op("vector", lambda h, dst=dst, Tp=Tcur, r=r, hd=hd: h.scalar_tensor_tensor(
                            out=dst, in0=Tp, scalar=SALL[:, r, hd, 128:129], in1=SALL[:, r, hd, 0:128],
                            op0=ALU.mult, op1=ALU.add), reads=[bSALL, bTCH], writes=[bTCH])
                        Tcur = dst
                    if first:
                        P.op("vector", lambda h, Tc=Tcur, r=r, hd=hd: h.tensor_scalar(
                            out=SST[hd][:, :], in0=Tc, scalar1=SEL[:, r + 1:r + 2], scalar2=0.0, op0=ALU.mult, op1=ALU.add),
                            reads=[bSALL, bTCH, bSEL], writes=[bSST[hd]])
                        first = False
                    else:
                        P.op("vector", lambda h, Tc=Tcur, r=r, hd=hd: h.scalar_tensor_tensor(
                            out=SST[hd][:, :], in0=Tc, scalar=SEL[:, r + 1:r + 2], in1=SST[hd][:, :],
                            op0=ALU.mult, op1=ALU.add), reads=[bSALL, bTCH, bSEL, bSST[hd]], writes=[bSST[hd]])
        P.alias([bPM, bHALO], [bSALL])
        P.dma("sync", lambda h: h.dma_start(out=PM, in_=pmat[:, :, :]), writes=[bPM])
        P.dma("sync", lambda h: h.dma_start(out=HALO, in_=halo[:, :]), writes=[bHALO])

        for t in range(NT if _STAGE >= 2 else 0):
            rmsnorm_to_HT(t)

        for t in range(NT if _STAGE >= 3 else 0):
            cur = t % 2
            for kc in range(8):
                P.op("tensor", lambda h, kc=kc, t=t: h.matmul(PS_F[:, 0:256], lhsT=HT[:, kc, t * 128:(t + 1) * 128],
                                                              rhs=WP[:, kc, :], start=(kc == 0), stop=(kc == 7)),
                     reads=[bHT[t], bWP], writes=[bPS_F])
            P.op("scalar", lambda h, cur=cur: h.activation(out=UP[:, cur, :], in_=PS_F[:, 0:256], func=AF.Copy),
                 reads=[bPS_F], writes=[bUP[cur]])
            prev_ap = HALO if t == 0 else UP[:, 1 - cur, :]
            prev_b = bHALO if t == 0 else bUP[1 - cur]
            for c2 in range(2 if _POOL >= 1 else 0):
                for wi in range(2):
                    g = 2 * c2 + wi
                    mcur = PM[:, (0 if t == 0 else 4) + g, :]
                    mprev = PM[:, 8 + g, :]
                    dst = PS_E[:, wi * 128:(wi + 1) * 128]
                    P.op("tensor", lambda h, dst=dst, mcur=mcur, cur=cur, c2=c2: h.matmul(
                        dst, lhsT=UP[:, cur, c2 * 128:(c2 + 1) * 128], rhs=mcur, start=True, stop=False),
                        reads=[bUP[cur], bPM], writes=[bPS_E])
                    P.op("tensor", lambda h, dst=dst, mprev=mprev, prev_ap=prev_ap, c2=c2: h.matmul(
                        dst, lhsT=prev_ap[:, c2 * 128:(c2 + 1) * 128], rhs=mprev, start=False, stop=True),
                        reads=[prev_b, bPM], writes=[bPS_E])
                if _POOL < 2:
                    continue
                P.op("scalar", lambda h: h.activation(out=PTB[0:64, :], in_=PS_E[0:64, 0:128], func=AF.Copy),
                     reads=[bPS_E], writes=[bPTB])
                P.op("scalar", lambda h: h.activation(out=PTB[64:128, :], in_=PS_E[64:128, 128:256], func=AF.Copy),
                     reads=[bPS_E], writes=[bPTB])
                if _POOL < 3:
                    continue
                P.op("tensor", lambda h, c2=c2: h.matmul(PS_D[:, 0:128], lhsT=PW2[:, c2, :], rhs=PTB[:, :],
                                                         start=True, stop=True),
                     reads=[bPW2, bPTB], writes=[bPS_Ds])
                if _POOL < 4:
                    continue
                P.op("vector", lambda h, c2=c2, t=t: h.tensor_scalar(out=YT[:, c2, t * 128:(t + 1) * 128],
                                                                     in0=PS_D[:, 0:128], scalar1=PSC[:, c2:c2 + 1],
                                                                     scalar2=0.0, op0=ALU.mult, op1=ALU.add),
                     reads=[bPS_Ds, bPSC], writes=[bYT[t]])

        for hd in range(NH if _STAGE >= 4 else 0):
            P.dma("gpsimd", lambda h, hd=hd: h.dma_start(out=WH[:, :, :], in_=w_heads[hd]), writes=[bWH])
            load_bcast(NGB[:, :], bNGB, ng_in[hd * 128:(hd + 1) * 128])

            def gate():
                P.op("scalar", lambda h: h.activation(out=EG[:, :, :], in_=PS_A[:, :, 256:384], func=AF.Exp, scale=-1.0),
                     reads=bPS_A, writes=[bEG])
                for j in range(2):
                    P.op("vector", lambda h, j=j: h.tensor_tensor(out=GG[:, j, :], in0=PS_A[:, j, 256:384],
                                                                  in1=NGB[:, :], op=ALU.mult),
                         reads=[bPS_A[j], bNGB], writes=[bGG])
                P.op("vector", lambda h: h.tensor_scalar(out=EG[:, :, :], in0=EG[:, :, :], scalar1=1.0, scalar2=1.0,
                                                         op0=ALU.add, op1=ALU.mult), reads=[bEG], writes=[bEG])
                P.op("vector", lambda h: h.reciprocal(out=EG[:, :, :], in_=EG[:, :, :]), reads=[bEG], writes=[bEG])
                P.op("vector", lambda h: h.tensor_tensor(out=GG[:, :, :], in0=GG[:, :, :], in1=EG[:, :, :], op=ALU.mult),
                     reads=[bEG, bGG], writes=[bGG])
                P.op("scalar", lambda h: h.activation(out=EQ[:, :, :], in_=v3(PS_B[:, 0:256], 2), func=AF.Exp, scale=-1.0),
                     reads=[bPS_Bq], writes=[bEQ])

            def qe():
                P.op("vector", lambda h: h.scalar_tensor_tensor(out=EQ[:, :, :], in0=EQ[:, :, :], scalar=1.0, in1=E2[:, :, :],
                                                                op0=ALU.add, op1=ALU.mult), reads=[bEQ, bE2], writes=[bEQ])
                P.op("vector", lambda h: h.reciprocal(out=EQ[:, :, :], in_=EQ[:, :, :]), reads=[bEQ], writes=[bEQ])
                P.op("vector", lambda h: h.tensor_tensor(out=QE[:, :, :], in0=v3(PS_B[:, 0:256], 2),
                                                         in1=EQ[:, :, :], op=ALU.mult),
                     reads=[bPS_Bq, bEQ], writes=[bQE])

            def out(j, t, hd=hd):
                P.op("vector", lambda h: h.tensor_scalar(out=S0B[:, :], in0=SST[hd][:, :], scalar1=EX[:, j, 0:1], scalar2=0.0,
                                                         op0=ALU.mult, op1=ALU.add), reads=[bSST[hd], bEX], writes=[bS0B])
                P.op("tensor", lambda h: h.matmul(PS_D[:, 0:128], lhsT=KE[:, j, :], rhs=QE[:, j, :], start=True, stop=True),
                     reads=[bKE, bQE], writes=[bPS_Ds])
                P.op("vector", lambda h: h.copy_predicated(out=SCM[:, j, :], mask=MSK[:, :], data=PS_D[:, 0:128]),
                     reads=[bPS_Ds, bMSK], writes=[bSCM])
                P.op("tensor", lambda h: h.matmul(PS_D[:, 256:384], lhsT=SCM[:, j, :], rhs=VV[:, j, :], start=True, stop=False),
                     reads=[bSCM, bVV], writes=[bPS_Do])
                P.op("tensor", lambda h: h.matmul(PS_D[:, 256:384], lhsT=QE[:, j, :], rhs=S0B[:, :], start=False, stop=True),
                     reads=[bQE, bS0B], writes=[bPS_Do])
                P.op("scalar", lambda h: h.activation(out=SQJ[:, :], in_=PS_D[:, 256:384], func=AF.Square,
                                                      accum_out=SS[:, 4:5]), reads=[bPS_Do], writes=[bSQJ, bSS])
                P.op("scalar", lambda h: h.activation(out=SS[:, 5:6], in_=SS[:, 4:5], func=AF.Ln, scale=1.0 / 128, bias=EPS),
                     reads=[bSS], writes=[bSS])
                P.op("scalar", lambda h: h.activation(out=SS[:, 6:7], in_=SS[:, 5:6], func=AF.Exp, scale=-0.5),
                     reads=[bSS], writes=[bSS])
                P.op("vector", lambda h: h.scalar_tensor_tensor(out=YY[:, :], in0=PS_D[:, 256:384], scalar=SS[:, 6:7],
                                                                in1=GG[:, j, :], op0=ALU.mult, op1=ALU.mult),
                     reads=[bPS_Do, bSS, bGG], writes=[bYY])
                P.op("tensor", lambda h: h.transpose(out=PS_T[:, 0, :], in_=YY[:, :], identity=IDB[:, :]),
                     reads=[bYY, bIDB], writes=[bPS_T])
                P.op("scalar", lambda h: h.activation(out=YT[:, 2 + hd, t * 128:(t + 1) * 128], in_=PS_T[:, 0, :], func=AF.Copy),
                     reads=[bPS_T], writes=[bYT[t]])

            scan_head(hd, True, 128, OMLB, bOMLB, extra={"gate": gate, "qe": qe, "out": out})

        if _DBG:
            dbg_y = dram_out("dbg_y", [128, 8, TOK], BF16)
            bDBG = P.buf("dbg")
            P.dma("sync", lambda h: h.dma_start(out=dbg_y[:, :, :], in_=YT), reads=bYT, sembuf=bDBG)
            store_bufs.append(bDBG)
        P.alias([bWO, bAT], [bPM, bHALO, bSALL])
        P.dma("gpsimd", lambda h: h.dma_start(out=WO, in_=w_out[:, :, :]), writes=[bWO])
        load_bcast(GB[:, :], bGB, g_mlp)
        pbank = [(PS_E[:, :], bPS_E), (PS_F[:, :], bPS_F)]
        for t in range(NT if _STAGE >= 5 else 0):
            for n in range(2):
                pt, pb = pbank[n]
                for kc in range(8):
                    P.op("tensor", lambda h, pt=pt, kc=kc, n=n, t=t: h.matmul(
                        pt, lhsT=YT[:, kc, t * 128:(t + 1) * 128], rhs=WO[:, kc, n * 512:(n + 1) * 512],
                        start=(kc == 0), stop=(kc == 7)), reads=[bYT[t], bWO], writes=[pb])
                P.op("vector", lambda h, pt=pt, n=n, t=t: h.tensor_tensor(
                    out=X[:, t, n * 512:(n + 1) * 512], in0=X[:, t, n * 512:(n + 1) * 512], in1=pt, op=ALU.add),
                    reads=[pb, bX[t]], writes=[bX[t]])
            rmsnorm_to_HT(t)

        P.alias([bWU, bWD], bYT)
        upb = [(PS_A[:, 0, :], bPS_A[0]), (PS_A[:, 1, :], bPS_A[1]), (PS_B[:, :], bPS_Bq), (PS_C[:, :], bPS_Cr)]
        dnb = [(PS_D[:, :], bPS_Ds), (PS_E[:, :], bPS_E), (PS_F[:, :], bPS_F)]
        ui = 0
        di = 0
        for blk in range(4 if _STAGE >= 6 else 0):
            P.dma("gpsimd", lambda h, blk=blk: h.dma_start(out=WU, in_=w_up[blk]), writes=[bWU])
            P.dma("gpsimd", lambda h, blk=blk: h.dma_start(out=WD, in_=w_dn[blk]), writes=[bWD])
            for tg in range(NT // 4):
                tok0 = tg * 512
                hts = [bHT[4 * tg + i] for i in range(4)]
                for m in range(8):
                    pt, pb = upb[ui % 4]; ui += 1
                    for kc in range(8):
                        P.op("tensor", lambda h, pt=pt, kc=kc, m=m, tok0=tok0: h.matmul(
                            pt, lhsT=WU[:, kc, m * 128:(m + 1) * 128], rhs=HT[:, kc, tok0:tok0 + 512],
                            start=(kc == 0), stop=(kc == 7)), reads=hts + [bWU], writes=[pb])
                    rl, brl = RL[:, ui % 2, :], bRL[ui % 2]
                    P.op("scalar", lambda h, pt=pt, rl=rl: h.activation(out=rl, in_=pt, func=AF.Relu),
                         reads=[pb], writes=[brl])
                    P.op("gpsimd", lambda h, m=m, rl=rl: h.tensor_tensor(out=AT[:, m, :], in0=rl, in1=rl, op=ALU.mult),
                         reads=[brl], writes=[bAT])
                for i in range(4):
                    t = 4 * tg + i
                    for n in range(2):
                        pt, pb = dnb[di % 3]; di += 1
                        for m in range(8):
                            P.op("tensor", lambda h, pt=pt, m=m, n=n, i=i: h.matmul(
                                pt, lhsT=AT[:, m, i * 128:(i + 1) * 128], rhs=WD[:, m, n * 512:(n + 1) * 512],
                                start=(m == 0), stop=(m == 7)), reads=[bAT, bWD], writes=[pb])
                        P.op("vector", lambda h, pt=pt, n=n, t=t: h.tensor_tensor(
                            out=X[:, t, n * 512:(n + 1) * 512], in0=X[:, t, n * 512:(n + 1) * 512], in1=pt, op=ALU.add),
                            reads=[pb, bX[t]], writes=[bX[t]])

    if do_A:
        load_bcast(GB[:, :], bGB, gA_mix)
        for t in range(NT):
            rmsnorm_to_HT(t)
        P.op("gpsimd", lambda h: h.memset(BTOT[:, :], 0.0), writes=[bBTOT])
        P.alias([bSOUT], [bSALL, bPM, bHALO, bWO, bAT])
        for hd in range(NH):
            P.op("gpsimd", lambda h, hd=hd: h.memset(SST[hd][:, :], 0.0), writes=[bSST[hd]])
            P.dma("gpsimd", lambda h, hd=hd: h.dma_start(out=WH[:, :, 0:256], in_=wA_fi[hd]), writes=[bWH])
            scan_head(hd, False, 0, OMLA, bOMLA)
            P.op("gpsimd", lambda h, hd=hd: h.tensor_copy(out=SOUT[:, hd, 0:128], in_=SST[hd][:, :]),
                 reads=[bSST[hd]], writes=[bSOUT])
        P.op("scalar", lambda h: h.activation(out=SOUT[:, :, 128], in_=BTOT[:, :], func=AF.Exp),
             reads=[bBTOT], writes=[bSOUT])
        bSO = P.buf("sloc_o")
        P.dma("sync", lambda h: h.dma_start(out=sloc_out.rearrange("h p f -> p h f"), in_=SOUT),
              reads=[bSOUT], sembuf=bSO)
        store_bufs.append(bSO)
        P.dma("gpsimd", lambda h: h.dma_start(out=WP[:, :, :], in_=wA_pool[:, :, :]), writes=[bWP])
        tl = NT - 1
        for kc in range(8):
            P.op("tensor", lambda h, kc=kc, t=tl: h.matmul(PS_F[:, 0:256], lhsT=HT[:, kc, t * 128:(t + 1) * 128],
                                                     rhs=WP[:, kc, :], start=(kc == 0), stop=(kc == 7)),
                 reads=[bHT[tl], bWP], writes=[bPS_F])
        P.op("scalar", lambda h: h.activation(out=UP[:, 0, :], in_=PS_F[:, 0:256], func=AF.Copy),
             reads=[bPS_F], writes=[bUP[0]])
        bHO = P.buf("halo_o")
        P.dma("sync", lambda h: h.dma_start(out=halo_out[:, :], in_=UP[:, 0, :]), reads=[bUP[0]], sembuf=bHO)
        store_bufs.append(bHO)
        if do_B:
            bXO = P.buf("x_o")
            xo_v = x_out.rearrange("(t p) d -> p t d", p=128)
            for t in range(NT):
                P.dma("sync", lambda h, t=t: h.dma_start(out=xo_v[:, t, :], in_=X[:, t, :]), reads=[bX[t]], sembuf=bXO)
            store_bufs.append(bXO)

    if do_final:
        load_bcast(GB[:, :], bGB, g_fin)
        YF = v3(HT[:, :, :].rearrange("p a b -> p (a b)").bitcast(F32)[:, 0:2 * D], 2)
        bYF = [P.buf("YF0"), P.buf("YF1")]
        P.alias(bYF, bHT)
        bYO = P.buf("y_o")
        yo_v = y_out.rearrange("(t p) d -> p t d", p=128)
        for t in range(NT):
            xt = X[:, t, :]
            rms_stats(xt, bX[t])
            P.op("vector", lambda h, xt=xt, t=t: h.scalar_tensor_tensor(out=YF[:, t % 2, :], in0=xt, scalar=SS[:, 2:3], in1=GB[:, :],
                                                                       op0=ALU.mult, op1=ALU.mult),
                 reads=[bX[t], bSS, bGB], writes=[bYF[t % 2]])
            P.dma("sync", lambda h, t=t: h.dma_start(out=yo_v[:, t, :], in_=YF[:, t % 2, :]), reads=[bYF[t % 2]], sembuf=bYO)
        store_bufs.append(bYO)

    P.finish("sync", store_bufs)
    P.run()
    stack.close()
    return nc


def _consts():
    s = np.arange(128)
    ident = np.eye(128, dtype=np.float32)
    tri = (s[:, None] <= s[None, :]).astype(np.float32)
    tm = tri - tri[:, MID:MID + 1]
    tmx = np.concatenate([tm, tri[:, MID:MID + 1], np.ones((128, 1), np.float32)], axis=1)
    U = (s[:, None] > s[None, :]).astype(np.float32)
    con = np.concatenate([ident, tmx, U], axis=1).astype(np.float32)
    mask = (s[:, None] <= s[None, :]).astype(np.int32)
    return np.ascontiguousarray(con), np.ascontiguousarray(mask)


def _pool_mats(first_segment):
    out = np.zeros((128, 12, 128), np.float32)
    for g, w in enumerate(WINS):
        for t in range(128):
            for s_ in range(t - w + 1, t + 1):
                if s_ >= 0:
                    out[s_, 4 + g, t] += 1.0 / w
                else:
                    out[128 + s_, 8 + g, t] += 1.0 / w
            out[t, 4 + g, t] -= 1.0
            if first_segment:
                cnt = min(t + 1, w)
                for s_ in range(max(0, t - w + 1), t + 1):
                    out[s_, g, t] += 1.0 / cnt
                out[t, g, t] -= 1.0
        if not first_segment:
            out[:, g, :] = out[:, 4 + g, :]
    return out


_PROGS = {}
_STAGE = 99
_POOL = 9
_DBG = False


def _prog(key):
    if key not in _PROGS:
        _PROGS[key] = build_program(*key)
    return _PROGS[key]


def kernel(x, norm_mix_g, w_in, pool_w, pool_scale, hgrn_lb_logits, hgrn_norm_g,
           w_out, norm_mlp_g, w_up, w_down, final_norm_g):
    f32 = np.float32
    x = np.asarray(x, f32)
    w_in = np.asarray(w_in, f32)
    con, mask = _consts()
    lbl = np.ascontiguousarray(np.asarray(hgrn_lb_logits, f32).reshape(-1))
    xs = [np.ascontiguousarray(x[c // 4, (c % 4) * TOK:(c % 4 + 1) * TOK, :]) for c in range(NCORES)]
    pm = [_pool_mats(c % 4 == 0) for c in range(NCORES)]
    sels = []
    for c in range(NCORES):
        s_ = np.zeros((128, NCORES), f32)
        if c % 4 != 0:
            s_[:, c] = 1.0
        sels.append(s_)

    def lmask(l):
        m = np.zeros((128, DEPTH), f32)
        m[:, 1:l + 1] = 1.0
        return m

    def kc_layout(w):
        K, N = w.shape
        return np.ascontiguousarray(w.reshape(K // 128, 128, N).transpose(1, 0, 2))

    def layer_B(l):
        wi = w_in[l]
        heads = []
        for hd in range(NH):
            cols = [wi[:, 256 + j * 768 + hd * 128: 256 + j * 768 + (hd + 1) * 128] for j in range(4)]
            heads.append(kc_layout(np.concatenate(cols, axis=1)))
        pw2 = np.zeros((2, 128, 128), f32)
        for g in range(4):
            c2, o = g // 2, (g % 2) * 64
            pw2[c2, o:o + 64, o:o + 64] = pool_w[l, g]
        wu = np.asarray(w_up[l], f32)
        wd = np.asarray(w_down[l], f32)
        return {
            "lmB": lmask(l),
            "g_mix": np.ascontiguousarray(norm_mix_g[l], f32),
            "g_mlp": np.ascontiguousarray(norm_mlp_g[l], f32),
            "w_heads": np.ascontiguousarray(np.stack(heads, 0)),
            "w_pool": kc_layout(wi[:, 0:256]),
            "pool_w2": pw2,
            "pool_sc": np.ascontiguousarray(np.asarray(pool_scale[l], f32).reshape(2, 128).T),
            "ng": np.ascontiguousarray(hgrn_norm_g[l], f32),
            "w_out": kc_layout(np.asarray(w_out[l], f32)),
            "w_up": np.ascontiguousarray(np.stack([kc_layout(wu[:, b * 1024:(b + 1) * 1024]) for b in range(4)], 0)),
            "w_dn": np.ascontiguousarray(np.stack([kc_layout(wd[b * 1024:(b + 1) * 1024, :]) for b in range(4)], 0)),
        }

    def layer_A(l):
        wi = w_in[l]
        fi = []
        for hd in range(NH):
            cols = [wi[:, 256 + j * 768 + hd * 128: 256 + j * 768 + (hd + 1) * 128] for j in (1, 2)]
            fi.append(kc_layout(np.concatenate(cols, axis=1)))
        return {
            "lmA": lmask(l),
            "gA_mix": np.ascontiguousarray(norm_mix_g[l], f32),
            "wA_fi": np.ascontiguousarray(np.stack(fi, 0)),
            "wA_pool": kc_layout(wi[:, 0:256]),
        }

    common = {"consts": con, "maskd": mask, "lbl": lbl}
    cores = list(range(NCORES))

    la = layer_A(0)
    res = run_bass_kernel_spmd(_prog((False, True, False)),
                               [dict(common, x_in=xs[c], **la) for c in cores], core_ids=cores).results
    for l in range(DEPTH):
        sall = np.ascontiguousarray(np.stack([res[c]["sloc"] for c in cores], 0))
        halos = [np.zeros((128, 256), f32) if c % 4 == 0 else np.ascontiguousarray(res[c - 1]["halo_o"]) for c in cores]
        lb_ = layer_B(l)
        la = layer_A(min(l + 1, DEPTH - 1))
        maps = [dict(common, x_in=xs[c], sall=sall, halo=halos[c], sel=sels[c], pmat=pm[c], **lb_, **la) for c in cores]
        res = run_bass_kernel_spmd(_prog((True, True, False)), maps, core_ids=cores).results
        xs = [np.ascontiguousarray(res[c]["x_out"]) for c in cores]
    gf = np.ascontiguousarray(final_norm_g, f32)
    res = run_bass_kernel_spmd(_prog((False, False, True)),
                               [dict(common, x_in=xs[c], g_fin=gf) for c in cores], core_ids=cores).results
    out = np.zeros((BATCH, SEQ, D), f32)
    for c in cores:
        out[c // 4, (c % 4) * TOK:(c % 4 + 1) * TOK, :] = res[c]["y_out"]
    return out
```

```python
import numpy as np
import concourse.bass as bass
import concourse.mybir as mybir
from concourse.bass_utils import run_bass_kernel_spmd

F32 = mybir.dt.float32
BF16 = mybir.dt.bfloat16
I32 = mybir.dt.int32
AF = mybir.ActivationFunctionType
ALU = mybir.AluOpType

NCORES = 8
D = 1024
SEQ = 8192
BATCH = 2
DEPTH = 4
TOK = 2048
NT = TOK // 128
NH = 6
DFF = 4096
EPS = 1e-5
MID = 63
WINS = (2, 4, 8, 16)


class Buf:
    def __init__(self, name):
        self.name = name
        self.last_write = None
        self.reads = {}
        self.dsem = None
        self.dcount = 0


class Prog:
    ENGS = ("tensor", "vector", "scalar", "gpsimd", "sync")

    def __init__(self, nc, stack):
        self.nc = nc
        self.stack = stack
        self.streams = {e: [] for e in self.ENGS}
        self.count = {e: 0 for e in self.ENGS}
        self.sem = {}
        for e in ("tensor", "vector", "scalar", "gpsimd"):
            self.sem[e] = stack.enter_context(nc.semaphore("s_" + e))
        self.seen = {e: {} for e in self.ENGS}
        self.nbuf = 0

    def buf(self, name):
        return Buf(name)

    def _dsem(self, b):
        if b.dsem is None:
            self.nbuf += 1
            b.dsem = self.stack.enter_context(self.nc.semaphore("d_%d" % self.nbuf))
        return b.dsem

    def _deps(self, eng, reads, writes):
        deps = []
        for b in reads:
            if b.last_write is not None:
                deps.append(b.last_write)
        for b in writes:
            if b.last_write is not None:
                deps.append(b.last_write)
            deps.extend(b.reads.values())
        out = []
        for d in deps:
            if d[0] == 'e':
                _, e2, idx = d
                if e2 == eng and eng == "tensor":
                    continue
                key = ('e', e2)
                sem = self.sem[e2]
            else:
                _, b2, idx = d
                key = ('d', id(b2))
                sem = b2.dsem
            if self.seen[eng].get(key, 0) >= idx:
                continue
            self.seen[eng][key] = idx
            out.append((sem, idx))
        return out

    def op(self, eng, fn, reads=(), writes=()):
        waits = self._deps(eng, reads, writes)
        sem = self.sem[eng]

        def emit(h, fn=fn, waits=waits, sem=sem):
            for s, v in waits:
                h.wait_ge(s, v)
            fn(h).then_inc(sem, 1)
        self.streams[eng].append(emit)
        self.count[eng] += 1
        idx = self.count[eng]
        for b in reads:
            b.reads[('e', eng)] = ('e', eng, idx)
        for b in writes:
            b.last_write = ('e', eng, idx)
            b.reads = {}

    def dma(self, q, fn, reads=(), writes=(), sembuf=None):
        waits = self._deps(q, reads, writes)
        sb = sembuf if sembuf is not None else (writes[0] if writes else reads[0])
        sem = self._dsem(sb)
        def emit(h, fn=fn, waits=waits, sem=sem):
            for s, v in waits:
                h.wait_ge(s, v)
            fn(h).then_inc(sem, 16)
        self.streams[q].append(emit)
        sb.dcount += 16
        dep = ('d', sb, sb.dcount)
        for b in reads:
            b.reads[('d', id(sb))] = dep
        for b in writes:
            b.last_write = dep
            b.reads = {}

    def alias(self, new_bufs, old_bufs):
        for nb in new_bufs:
            for ob in old_bufs:
                deps = list(ob.reads.items())
                if ob.last_write is not None:
                    d = ob.last_write
                    deps.append(((d[0], d[1] if d[0] == 'e' else id(d[1])), d))
                for k, d in deps:
                    cur = nb.reads.get(k)
                    if cur is None or cur[2] < d[2]:
                        nb.reads[k] = d

    def finish(self, q, bufs):
        lst = [(b.dsem, b.dcount) for b in bufs if b.dsem is not None]

        def emit(h):
            for s, v in lst:
                h.wait_ge(s, v)
        self.streams[q].append(emit)

    def run(self):
        nc = self.nc
        with nc.Block() as block:
            @block.tensor
            def _(h):
                for f in self.streams["tensor"]:
                    f(h)

            @block.vector
            def _(h):
                for f in self.streams["vector"]:
                    f(h)

            @block.scalar
            def _(h):
                for f in self.streams["scalar"]:
                    f(h)

            @block.gpsimd
            def _(h):
                for f in self.streams["gpsimd"]:
                    f(h)

            @block.sync
            def _(h):
                for f in self.streams["sync"]:
                    f(h)


def build_program(do_B, do_A, do_final, nA=NH):
    from contextlib import ExitStack
    nc = bass.Bass("TRN2", target_bir_lowering=False)
    stack = ExitStack()
    P = Prog(nc, stack)

    def dram_in(name, shape, dt=F32):
        return nc.dram_tensor(name, list(shape), dt, kind="ExternalInput").ap()

    def dram_out(name, shape, dt=F32):
        return nc.dram_tensor(name, list(shape), dt, kind="ExternalOutput").ap()

    def sb(name, shape, dt):
        return stack.enter_context(nc.sbuf_tensor(name, list(shape), dt))

    def ps(name, shape, dt):
        return stack.enter_context(nc.psum_tensor(name, list(shape), dt))

    def v3(ap, a):
        return ap.rearrange("p (a b) -> p a b", a=a)

    x_in = dram_in("x_in", [TOK, D])
    consts = dram_in("consts", [128, 386])
    maskd = dram_in("maskd", [128, 128], I32)
    lbl = dram_in("lbl", [DEPTH * NH * 128])
    if do_B:
        lmB = dram_in("lmB", [128, DEPTH])
        g_mix = dram_in("g_mix", [D])
        g_mlp = dram_in("g_mlp", [D])
        w_heads = dram_in("w_heads", [NH, 128, 8, 512])
        w_pool = dram_in("w_pool", [128, 8, 256])
        pool_w2 = dram_in("pool_w2", [2, 128, 128])
        pool_sc = dram_in("pool_sc", [128, 2])
        pmat = dram_in("pmat", [128, 12, 128])
        halo = dram_in("halo", [128, 256])
        ng_in = dram_in("ng", [NH * 128])
        w_out = dram_in("w_out", [128, 8, D])
        w_up = dram_in("w_up", [4, 128, 8, 1024])
        w_dn = dram_in("w_dn", [4, 128, 8, D])
        sall = dram_in("sall", [NCORES, NH, 128, 129])
        sel = dram_in("sel", [128, NCORES])
    if do_A:
        lmA = dram_in("lmA", [128, DEPTH])
        gA_mix = dram_in("gA_mix", [D])
        wA_fi = dram_in("wA_fi", [NH, 128, 8, 256])
        wA_pool = dram_in("wA_pool", [128, 8, 256])
        sloc_out = dram_out("sloc", [NH, 128, 129])
        halo_out = dram_out("halo_o", [128, 256])
        if do_B:
            x_out = dram_out("x_out", [TOK, D])
    if do_final:
        g_fin = dram_in("g_fin", [D])
        y_out = dram_out("y_out", [TOK, D])

    X = sb("X", [128, NT, D], F32)
    HT = sb("HT", [128, 8, TOK], BF16)
    R1 = sb("R1", [128, 16384], BF16)
    R2 = sb("R2", [128, 6400], F32)
    R1f = R1[:, :].bitcast(F32)
    R2b = R2[:, :].bitcast(BF16)
    YT = v3(R1[:, :], 8)
    WU = v3(R1[:, 0:8192], 8)
    WD = v3(R1[:, 8192:16384], 8)
    LG = v3(R1f[:, 0:3072], 4)
    TS = R1f[:, 3072:3840]
    TN = R1f[:, 3840:4608]
    SALL = R2[:, 0:6192].rearrange("p (r h f) -> p r h f", r=NCORES, h=NH)
    PM = v3(R2[:, 0:1536], 12)
    HALO = R2[:, 1536:1792]
    WO = v3(R2b[:, 0:8192], 8)
    AT = v3(R2b[:, 8192:12288], 8)
    SOUT = v3(R2[:, 0:774], NH)

    bX = [P.buf("X%d" % t) for t in range(NT)]
    bHT = [P.buf("HT%d" % t) for t in range(NT)]
    bYT = [P.buf("YT%d" % t) for t in range(NT)]
    bWU = P.buf("WU"); bWD = P.buf("WD"); bLG = P.buf("LG")
    bSALL = P.buf("SALL"); bPM = P.buf("PM"); bHALO = P.buf("HALO"); bWO = P.buf("WO"); bAT = P.buf("AT")
    bSOUT = P.buf("SOUT")

    CON = sb("CON", [128, 386], F32); bCON = P.buf("CON")
    IDB = sb("IDB", [128, 128], BF16); bIDB = P.buf("IDB")
    MSK = sb("MSK", [128, 128], I32); bMSK = P.buf("MSK")
    ident = CON[:, 0:128]
    TmX = CON[:, 128:258]
    Umat = CON[:, 258:386]
    GB = sb("GB", [128, D], F32); bGB = P.buf("GB")
    SQJ = sb("SQJ", [128, 128], BF16); bSQJ = P.buf("SQJ")
    H0 = sb("H0", [128, D], BF16); bH0 = P.buf("H0")
    SS = sb("SS", [128, 8], F32); bSS = P.buf("SS")
    LM = sb("LM", [128, 2 * DEPTH], F32); bLM = P.buf("LM")
    OMLB = sb("OMLB", [128, NH * 128], F32); bOMLB = P.buf("OMLB")
    OMLA = sb("OMLA", [128, NH * 128], F32); bOMLA = P.buf("OMLA")
    EZ = sb("EZ", [128, 2, 128], F32); bEZ = P.buf("EZ")
    KK = sb("KK", [128, 2, 128], F32); bKK = P.buf("KK")
    LF = sb("LF", [128, 2, 128], F32); bLF = P.buf("LF")
    VV = sb("VV", [128, 2, 128], BF16); bVV = P.buf("VV")
    E2 = sb("E2", [128, 2, 128], F32); bE2 = P.buf("E2")
    ED = sb("ED", [128, 2, 128], F32); bED = P.buf("ED")
    EX = sb("EX", [128, 2, 2], F32); bEX = P.buf("EX")
    KE = sb("KE", [128, 2, 128], BF16); bKE = P.buf("KE")
    KD = sb("KD", [128, 2, 128], BF16); bKD = P.buf("KD")
    SST = [sb("SST%d" % h, [128, 128], F32) for h in range(NH)]
    bSST = [P.buf("SST%d" % h) for h in range(NH)]
    BTOT = sb("BTOT", [128, NH], F32); bBTOT = P.buf("BTOT")
    WH = sb("WH", [128, 8, 512], BF16); bWH = P.buf("WH")
    WP = sb("WP", [128, 8, 256], BF16); bWP = P.buf("WP")
    UP = sb("UP", [128, 2, 256], F32); bUP = [P.buf("UP0"), P.buf("UP1")]

    PS_T = ps("PS_T", [128, 8, 128], BF16); bPS_T = P.buf("PS_T")
    PS_A = ps("PS_A", [128, 2, 512], F32); bPS_A = [P.buf("PS_A0"), P.buf("PS_A1")]
    PS_B = ps("PS_B", [128, 512], F32); bPS_Bq = P.buf("PS_B"); bPS_Bb = bPS_Bq
    PS_C = ps("PS_C", [128, 512], F32); bPS_Cr = P.buf("PS_C"); bPS_Ck = bPS_Cr
    PS_D = ps("PS_D", [128, 512], F32); bPS_Ds = P.buf("PS_D"); bPS_Do = bPS_Ds
    PS_E = ps("PS_E", [128, 512], F32); bPS_E = P.buf("PS_E"); bPS_Ex = bPS_E
    PS_F = ps("PS_F", [128, 512], F32); bPS_F = P.buf("PS_F"); bPS_F2 = bPS_F

    x_v = x_in.rearrange("(t p) d -> p t d", p=128)
    for t in range(NT):
        P.dma("sync", lambda h, t=t: h.dma_start(out=X[:, t, :], in_=x_v[:, t, :]), writes=[bX[t]])
    P.dma("sync", lambda h: h.dma_start(out=CON[:, :], in_=consts[:, :]), writes=[bCON])
    P.dma("sync", lambda h: h.dma_start(out=MSK[:, :], in_=maskd[:, :]), writes=[bMSK])
    P.dma("gpsimd", lambda h: h.dma_start(out=IDB[:, :], in_=consts[:, 0:128]), writes=[bIDB])

    def load_bcast(dst, bdst, src):
        P.dma("sync", lambda h: h.dma_start(out=dst, in_=src.partition_broadcast(128)), writes=[bdst])

    load_bcast(R1f[:, 0:3072], bLG, lbl)
    if do_B:
        P.dma("sync", lambda h: h.dma_start(out=LM[:, 0:DEPTH], in_=lmB[:, :]), writes=[bLM])
    if do_A:
        P.dma("sync", lambda h: h.dma_start(out=LM[:, DEPTH:2 * DEPTH], in_=lmA[:, :]), writes=[bLM])
    P.op("scalar", lambda h: h.activation(out=R1f[:, 0:3072], in_=R1f[:, 0:3072], func=AF.Exp), reads=[bLG], writes=[bLG])
    P.op("vector", lambda h: h.tensor_tensor(out=TS, in0=LG[:, 0, :], in1=LG[:, 1, :], op=ALU.add), reads=[bLG], writes=[bLG])
    P.op("vector", lambda h: h.tensor_tensor(out=TS, in0=TS, in1=LG[:, 2, :], op=ALU.add), reads=[bLG], writes=[bLG])
    P.op("vector", lambda h: h.tensor_tensor(out=TS, in0=TS, in1=LG[:, 3, :], op=ALU.add), reads=[bLG], writes=[bLG])
    P.op("vector", lambda h: h.reciprocal(out=TS, in_=TS), reads=[bLG], writes=[bLG])

    def compute_oml(dst, bdst, off):
        P.op("vector", lambda h: h.tensor_scalar(out=TN, in0=LG[:, 0, :], scalar1=LM[:, off:off + 1], scalar2=0.0,
                                                 op0=ALU.mult, op1=ALU.add), reads=[bLG, bLM], writes=[bLG])
        for j in range(1, DEPTH):
            P.op("vector", lambda h, j=j: h.scalar_tensor_tensor(out=TN, in0=LG[:, j, :], scalar=LM[:, off + j:off + j + 1],
                                                                 in1=TN, op0=ALU.mult, op1=ALU.add),
                 reads=[bLG, bLM], writes=[bLG])
        P.op("vector", lambda h: h.tensor_tensor(out=TN, in0=TN, in1=TS, op=ALU.mult), reads=[bLG], writes=[bLG])
        P.op("vector", lambda h: h.tensor_scalar(out=dst, in0=TN, scalar1=-1.0, scalar2=1.0, op0=ALU.mult, op1=ALU.add),
             reads=[bLG], writes=[bdst])
    if do_B:
        compute_oml(OMLB[:, :], bOMLB, 0)
    if do_A:
        compute_oml(OMLA[:, :], bOMLA, DEPTH)
    P.alias(bYT + [bWU, bWD], [bLG])

    def rms_stats(xt, bx):
        P.op("scalar", lambda h: h.activation(out=H0[:, :], in_=xt, func=AF.Square, accum_out=SS[:, 0:1]),
             reads=[bx], writes=[bH0, bSS])
        P.op("scalar", lambda h: h.activation(out=SS[:, 1:2], in_=SS[:, 0:1], func=AF.Ln, scale=1.0 / D, bias=EPS),
             reads=[bSS], writes=[bSS])
        P.op("scalar", lambda h: h.activation(out=SS[:, 2:3], in_=SS[:, 1:2], func=AF.Exp, scale=-0.5),
             reads=[bSS], writes=[bSS])

    def rmsnorm_to_HT(t):
        xt = X[:, t, :]
        rms_stats(xt, bX[t])
        P.op("vector", lambda h: h.scalar_tensor_tensor(out=H0[:, :], in0=xt, scalar=SS[:, 2:3], in1=GB[:, :],
                                                        op0=ALU.mult, op1=ALU.mult),
             reads=[bX[t], bSS, bGB], writes=[bH0])
        for kc in range(8):
            P.op("tensor", lambda h, kc=kc: h.transpose(out=PS_T[:, kc, :], in_=H0[:, kc * 128:(kc + 1) * 128],
                                                        identity=IDB[:, :]),
                 reads=[bH0, bIDB], writes=[bPS_T])
        P.op("scalar", lambda h: h.activation(out=HT[:, :, t * 128:(t + 1) * 128], in_=PS_T[:, :, :], func=AF.Copy),
             reads=[bPS_T], writes=[bHT[t]])

    def scan_head(hd, full, woff, OML, bOML, extra=None):
        ncol = 384 if full else 256
        for g0 in range(0, NT, 2):
            tiles = (g0, g0 + 1)
            for j, t in enumerate(tiles):
                for kc in range(8):
                    P.op("tensor", lambda h, j=j, t=t, kc=kc: h.matmul(
                        PS_A[:, j, 0:ncol], lhsT=HT[:, kc, t * 128:(t + 1) * 128],
                        rhs=WH[:, kc, woff:woff + ncol], start=(kc == 0), stop=(kc == 7)),
                        reads=[bHT[t], bWH], writes=[bPS_A[j]])
            if full:
                for kc in range(8):
                    P.op("tensor", lambda h, kc=kc, g0=g0: h.matmul(
                        PS_B[:, 0:256], lhsT=WH[:, kc, 0:128], rhs=HT[:, kc, g0 * 128:(g0 + 2) * 128],
                        start=(kc == 0), stop=(kc == 7)),
                        reads=[bHT[g0], bHT[g0 + 1], bWH], writes=[bPS_Bq])
            P.op("scalar", lambda h: h.activation(out=EZ[:, :, :], in_=PS_A[:, :, 0:128], func=AF.Exp),
                 reads=bPS_A, writes=[bEZ])
            P.op("gpsimd", lambda h: h.tensor_scalar(out=EZ[:, :, :], in0=EZ[:, :, :], scalar1=1.0, scalar2=1.0,
                                                     op0=ALU.add, op1=ALU.mult), reads=[bEZ], writes=[bEZ])
            P.op("vector", lambda h: h.reciprocal(out=EZ[:, :, :], in_=EZ[:, :, :]), reads=[bEZ], writes=[bEZ])
            for j in range(2):
                P.op("vector", lambda h, j=j: h.tensor_tensor(out=KK[:, j, :], in0=EZ[:, j, :],
                                                              in1=OML[:, hd * 128:(hd + 1) * 128], op=ALU.mult),
                     reads=[bEZ, bOML], writes=[bKK])
            P.op("scalar", lambda h: h.activation(out=LF[:, :, :], in_=KK[:, :, :], func=AF.Ln, scale=-1.0, bias=1.0),
                 reads=[bKK], writes=[bLF])
            P.op("scalar", lambda h: h.activation(out=VV[:, :, :], in_=PS_A[:, :, 128:256], func=AF.Copy),
                 reads=bPS_A, writes=[bVV])
            if full:
                extra["gate"]()
            for j in range(2):
                P.op("tensor", lambda h, j=j: h.matmul(PS_B[:, 256 + j * 128:256 + (j + 1) * 128], lhsT=LF[:, j, :],
                                                       rhs=TmX[:, 0:128], start=True, stop=True),
                     reads=[bLF, bCON], writes=[bPS_Bb])
                P.op("tensor", lambda h, j=j: h.matmul(PS_E[:, 256 + j * 2:256 + j * 2 + 2], lhsT=LF[:, j, :],
                                                       rhs=TmX[:, 128:130], start=True, stop=True),
                     reads=[bLF, bCON], writes=[bPS_Ex])
                P.op("tensor", lambda h, j=j: h.matmul(PS_C[:, j * 128:(j + 1) * 128], lhsT=Umat, rhs=LF[:, j, :],
                                                       start=True, stop=True),
                     reads=[bLF, bCON], writes=[bPS_Cr])
                if full:
                    P.op("tensor", lambda h, j=j: h.matmul(PS_C[:, 256 + j * 128:256 + (j + 1) * 128], lhsT=KK[:, j, :],
                                                           rhs=ident, start=True, stop=True),
                         reads=[bKK, bCON], writes=[bPS_Ck])
            P.op("scalar", lambda h: h.activation(out=ED[:, :, :], in_=v3(PS_C[:, 0:256], 2), func=AF.Exp),
                 reads=[bPS_Cr], writes=[bED])
            P.op("scalar", lambda h: h.activation(out=EX[:, :, :], in_=v3(PS_E[:, 256:260], 2), func=AF.Exp),
                 reads=[bPS_Ex], writes=[bEX])
            if not full:
                for j in range(2):
                    P.op("vector", lambda h, j=j: h.tensor_tensor(out=BTOT[:, hd:hd + 1], in0=BTOT[:, hd:hd + 1],
                                                                  in1=PS_E[:, 256 + 2 * j + 1:256 + 2 * j + 2], op=ALU.add),
                         reads=[bPS_Ex, bBTOT], writes=[bBTOT])
            P.op("vector", lambda h: h.tensor_tensor(out=KD[:, :, :], in0=KK[:, :, :], in1=ED[:, :, :], op=ALU.mult),
                 reads=[bKK, bED], writes=[bKD])
            if full:
                P.op("scalar", lambda h: h.activation(out=E2[:, :, :], in_=v3(PS_B[:, 256:512], 2), func=AF.Exp, scale=-1.0),
                     reads=[bPS_Bb], writes=[bE2])
                P.op("vector", lambda h: h.tensor_tensor(out=KE[:, :, :], in0=v3(PS_C[:, 256:512], 2),
                                                         in1=E2[:, :, :], op=ALU.mult),
                     reads=[bPS_Ck, bE2], writes=[bKE])
                extra["qe"]()
            for j, t in enumerate(tiles):
                if full:
                    extra["out"](j, t)
                P.op("tensor", lambda h, j=j: h.matmul(PS_E[:, 0:128], lhsT=KD[:, j, :], rhs=VV[:, j, :],
                                                       start=True, stop=True),
                     reads=[bKD, bVV], writes=[bPS_E])
                P.op("vector", lambda h, j=j: h.scalar_tensor_tensor(out=SST[hd][:, :], in0=SST[hd][:, :],
                                                                     scalar=EX[:, j, 1:2], in1=PS_E[:, 0:128],
                                                                     op0=ALU.mult, op1=ALU.add),
                     reads=[bSST[hd], bEX, bPS_E], writes=[bSST[hd]])

    store_bufs = []

    if do_B:
        SEL = sb("SEL", [128, NCORES], F32); bSEL = P.buf("SEL")
        TCH = sb("TCH", [128, 2, 128], F32); bTCH = P.buf("TCH")
        NGB = sb("NGB", [128, 128], F32); bNGB = P.buf("NGB")
        EG = sb("EG", [128, 2, 128], F32); bEG = P.buf("EG")
        GG = sb("GG", [128, 2, 128], F32); bGG = P.buf("GG")
        EQ = sb("EQ", [128, 2, 128], F32); bEQ = P.buf("EQ")
        QE = sb("QE", [128, 2, 128], BF16); bQE = P.buf("QE")
        SCM = sb("SCM", [128, 2, 128], BF16); bSCM = P.buf("SCM")
        S0B = sb("S0B", [128, 128], BF16); bS0B = P.buf("S0B")
        YY = sb("YY", [128, 128], BF16); bYY = P.buf("YY")
        PW2 = sb("PW2", [128, 2, 128], BF16); bPW2 = P.buf("PW2")
        PSC = sb("PSC", [128, 2], F32); bPSC = P.buf("PSC")
        PTB = sb("PTB", [128, 128], BF16); bPTB = P.buf("PTB")
        RL = sb("RL", [128, 2, 512], F32); bRL = [P.buf("RL0"), P.buf("RL1")]

        load_bcast(GB[:, :], bGB, g_mix)
        for r in range(NCORES):
            P.dma("sync", lambda h, r=r: h.dma_start(out=SALL[:, r, :, :], in_=sall[r].rearrange("h p f -> p h f")), writes=[bSALL])
        P.dma("sync", lambda h: h.dma_start(out=SEL[:, :], in_=sel[:, :]), writes=[bSEL])
        P.dma("sync", lambda h: h.dma_start(out=PSC[:, :], in_=pool_sc[:, :]), writes=[bPSC])
        P.dma("gpsimd", lambda h: h.dma_start(out=PW2[:, :, :], in_=pool_w2.rearrange("c p d -> p c d")), writes=[bPW2])
        P.dma("gpsimd", lambda h: h.dma_start(out=WP[:, :, :], in_=w_pool[:, :, :]), writes=[bWP])
        P.op("gpsimd", lambda h: h.memset(SCM[:, :, :], 0.0), writes=[bSCM])

        for hd in range(NH if _STAGE >= 1 else 0):
            first = True
            for base in (0, 4):
                Tcur = None
                for r in range(base, base + 3):
                    if r == base:
                        Tcur = SALL[:, r, hd, 0:128]
                    else:
                        dst = TCH[:, r % 2, :]
                        P.op("vector", lambda h, dst=dst, Tp=Tcur, r=r, hd=hd: h.scalar_tensor_tensor(
                            out=dst, in0=Tp, scalar=SALL[:, r, hd, 128:129], in1=SALL[:, r, hd, 0:128],
                            op0=ALU.mult, op1=ALU.add), reads=[bSALL, bTCH], writes=[bTCH])
                        Tcur = dst
                    if first:
                        P.op("vector", lambda h, Tc=Tcur, r=r, hd=hd: h.tensor_scalar(
                            out=SST[hd][:, :], in0=Tc, scalar1=SEL[:, r + 1:r + 2], scalar2=0.0, op0=ALU.mult, op1=ALU.add),
                            reads=[bSALL, bTCH, bSEL], writes=[bSST[hd]])
                        first = False
                    else:
                        P.op("vector", lambda h, Tc=Tcur, r=r, hd=hd: h.scalar_tensor_tensor(
                            out=SST[hd][:, :], in0=Tc, scalar=SEL[:, r + 1:r + 2], in1=SST[hd][:, :],
                            op0=ALU.mult, op1=ALU.add), reads=[bSALL, bTCH, bSEL, bSST[hd]], writes=[bSST[hd]])
        P.alias([bPM, bHALO], [bSALL])
        P.dma("sync", lambda h: h.dma_start(out=PM, in_=pmat[:, :, :]), writes=[bPM])
        P.dma("sync", lambda h: h.dma_start(out=HALO, in_=halo[:, :]), writes=[bHALO])

        for t in range(NT if _STAGE >= 2 else 0):
            rmsnorm_to_HT(t)

        for t in range(NT if _STAGE >= 3 else 0):
            cur = t % 2
            for kc in range(8):
                P.op("tensor", lambda h, kc=kc, t=t: h.matmul(PS_F[:, 0:256], lhsT=HT[:, kc, t * 128:(t + 1) * 128],
                                                              rhs=WP[:, kc, :], start=(kc == 0), stop=(kc == 7)),
                     reads=[bHT[t], bWP], writes=[bPS_F])
            P.op("scalar", lambda h, cur=cur: h.activation(out=UP[:, cur, :], in_=PS_F[:, 0:256], func=AF.Copy),
                 reads=[bPS_F], writes=[bUP[cur]])
            prev_ap = HALO if t == 0 else UP[:, 1 - cur, :]
            prev_b = bHALO if t == 0 else bUP[1 - cur]
            for c2 in range(2 if _POOL >= 1 else 0):
                for wi in range(2):
                    g = 2 * c2 + wi
                    mcur = PM[:, (0 if t == 0 else 4) + g, :]
                    mprev = PM[:, 8 + g, :]
                    dst = PS_E[:, wi * 128:(wi + 1) * 128]
                    P.op("tensor", lambda h, dst=dst, mcur=mcur, cur=cur, c2=c2: h.matmul(
                        dst, lhsT=UP[:, cur, c2 * 128:(c2 + 1) * 128], rhs=mcur, start=True, stop=False),
                        reads=[bUP[cur], bPM], writes=[bPS_E])
                    P.op("tensor", lambda h, dst=dst, mprev=mprev, prev_ap=prev_ap, c2=c2: h.matmul(
                        dst, lhsT=prev_ap[:, c2 * 128:(c2 + 1) * 128], rhs=mprev, start=False, stop=True),
                        reads=[prev_b, bPM], writes=[bPS_E])
                if _POOL < 2:
                    continue
                P.op("scalar", lambda h: h.activation(out=PTB[0:64, :], in_=PS_E[0:64, 0:128], func=AF.Copy),
                     reads=[bPS_E], writes=[bPTB])
                P.op("scalar", lambda h: h.activation(out=PTB[64:128, :], in_=PS_E[64:128, 128:256], func=AF.Copy),
                     reads=[bPS_E], writes=[bPTB])
                if _POOL < 3:
                    continue
                P.op("tensor", lambda h, c2=c2: h.matmul(PS_D[:, 0:128], lhsT=PW2[:, c2, :], rhs=PTB[:, :],
                                                         start=True, stop=True),
                     reads=[bPW2, bPTB], writes=[bPS_Ds])
                if _POOL < 4:
                    continue
                P.op("vector", lambda h, c2=c2, t=t: h.tensor_scalar(out=YT[:, c2, t * 128:(t + 1) * 128],
                                                                     in0=PS_D[:, 0:128], scalar1=PSC[:, c2:c2 + 1],
                                                                     scalar2=0.0, op0=ALU.mult, op1=ALU.add),
                     reads=[bPS_Ds, bPSC], writes=[bYT[t]])

        for hd in range(NH if _STAGE >= 4 else 0):
            P.dma("gpsimd", lambda h, hd=hd: h.dma_start(out=WH[:, :, :], in_=w_heads[hd]), writes=[bWH])
            load_bcast(NGB[:, :], bNGB, ng_in[hd * 128:(hd + 1) * 128])

            def gate():
                P.op("scalar", lambda h: h.activation(out=EG[:, :, :], in_=PS_A[:, :, 256:384], func=AF.Exp, scale=-1.0),
                     reads=bPS_A, writes=[bEG])
                for j in range(2):
                    P.op("vector", lambda h, j=j: h.tensor_tensor(out=GG[:, j, :], in0=PS_A[:, j, 256:384],
                                                                  in1=NGB[:, :], op=ALU.mult),
                         reads=[bPS_A[j], bNGB], writes=[bGG])
                P.op("vector", lambda h: h.tensor_scalar(out=EG[:, :, :], in0=EG[:, :, :], scalar1=1.0, scalar2=1.0,
                                                         op0=ALU.add, op1=ALU.mult), reads=[bEG], writes=[bEG])
                P.op("vector", lambda h: h.reciprocal(out=EG[:, :, :], in_=EG[:, :, :]), reads=[bEG], writes=[bEG])
                P.op("vector", lambda h: h.tensor_tensor(out=GG[:, :, :], in0=GG[:, :, :], in1=EG[:, :, :], op=ALU.mult),
                     reads=[bEG, bGG], writes=[bGG])
                P.op("scalar", lambda h: h.activation(out=EQ[:, :, :], in_=v3(PS_B[:, 0:256], 2), func=AF.Exp, scale=-1.0),
                     reads=[bPS_Bq], writes=[bEQ])

            def qe():
                P.op("vector", lambda h: h.scalar_tensor_tensor(out=EQ[:, :, :], in0=EQ[:, :, :], scalar=1.0, in1=E2[:, :, :],
                                                                op0=ALU.add, op1=ALU.mult), reads=[bEQ, bE2], writes=[bEQ])
                P.op("vector", lambda h: h.reciprocal(out=EQ[:, :, :], in_=EQ[:, :, :]), reads=[bEQ], writes=[bEQ])
                P.op("vector", lambda h: h.tensor_tensor(out=QE[:, :, :], in0=v3(PS_B[:, 0:256], 2),
                                                         in1=EQ[:, :, :], op=ALU.mult),
                     reads=[bPS_Bq, bEQ], writes=[bQE])

            def out(j, t, hd=hd):
                P.op("vector", lambda h: h.tensor_scalar(out=S0B[:, :], in0=SST[hd][:, :], scalar1=EX[:, j, 0:1], scalar2=0.0,
                                                         op0=ALU.mult, op1=ALU.add), reads=[bSST[hd], bEX], writes=[bS0B])
                P.op("tensor", lambda h: h.matmul(PS_D[:, 0:128], lhsT=KE[:, j, :], rhs=QE[:, j, :], start=True, stop=True),
                     reads=[bKE, bQE], writes=[bPS_Ds])
                P.op("vector", lambda h: h.copy_predicated(out=SCM[:, j, :], mask=MSK[:, :], data=PS_D[:, 0:128]),
                     reads=[bPS_Ds, bMSK], writes=[bSCM])
                P.op("tensor", lambda h: h.matmul(PS_D[:, 256:384], lhsT=SCM[:, j, :], rhs=VV[:, j, :], start=True, stop=False),
                     reads=[bSCM, bVV], writes=[bPS_Do])
                P.op("tensor", lambda h: h.matmul(PS_D[:, 256:384], lhsT=QE[:, j, :], rhs=S0B[:, :], start=False, stop=True),
                     reads=[bQE, bS0B], writes=[bPS_Do])
                P.op("scalar", lambda h: h.activation(out=SQJ[:, :], in_=PS_D[:, 256:384], func=AF.Square,
                                                      accum_out=SS[:, 4:5]), reads=[bPS_Do], writes=[bSQJ, bSS])
                P.op("scalar", lambda h: h.activation(out=SS[:, 5:6], in_=SS[:, 4:5], func=AF.Ln, scale=1.0 / 128, bias=EPS),
                     reads=[bSS], writes=[bSS])
                P.op("scalar", lambda h: h.activation(out=SS[:, 6:7], in_=SS[:, 5:6], func=AF.Exp, scale=-0.5),
                     reads=[bSS], writes=[bSS])
                P.op("vector", lambda h: h.scalar_tensor_tensor(out=YY[:, :], in0=PS_D[:, 256:384], scalar=SS[:, 6:7],
                                                                in1=GG[:, j, :], op0=ALU.mult, op1=ALU.mult),
                     reads=[bPS_Do, bSS, bGG], writes=[bYY])
                P.op("tensor", lambda h: h.transpose(out=PS_T[:, 0, :], in_=YY[:, :], identity=IDB[:, :]),
                     reads=[bYY, bIDB], writes=[bPS_T])
                P.op("scalar", lambda h: h.activation(out=YT[:, 2 + hd, t * 128:(t + 1) * 128], in_=PS_T[:, 0, :], func=AF.Copy),
                     reads=[bPS_T], writes=[bYT[t]])

            scan_head(hd, True, 128, OMLB, bOMLB, extra={"gate": gate, "qe": qe, "out": out})

        if _DBG:
            dbg_y = dram_out("dbg_y", [128, 8, TOK], BF16)
            bDBG = P.buf("dbg")
            P.dma("sync", lambda h: h.dma_start(out=dbg_y[:, :, :], in_=YT), reads=bYT, sembuf=bDBG)
            store_bufs.append(bDBG)
        P.alias([bWO, bAT], [bPM, bHALO, bSALL])
        P.dma("gpsimd", lambda h: h.dma_start(out=WO, in_=w_out[:, :, :]), writes=[bWO])
        load_bcast(GB[:, :], bGB, g_mlp)
        pbank = [(PS_E[:, :], bPS_E), (PS_F[:, :], bPS_F)]
        for t in range(NT if _STAGE >= 5 else 0):
            for n in range(2):
                pt, pb = pbank[n]
                for kc in range(8):
                    P.op("tensor", lambda h, pt=pt, kc=kc, n=n, t=t: h.matmul(
                        pt, lhsT=YT[:, kc, t * 128:(t + 1) * 128], rhs=WO[:, kc, n * 512:(n + 1) * 512],
                        start=(kc == 0), stop=(kc == 7)), reads=[bYT[t], bWO], writes=[pb])
                P.op("vector", lambda h, pt=pt, n=n, t=t: h.tensor_tensor(
                    out=X[:, t, n * 512:(n + 1) * 512], in0=X[:, t, n * 512:(n + 1) * 512], in1=pt, op=ALU.add),
                    reads=[pb, bX[t]], writes=[bX[t]])
            rmsnorm_to_HT(t)

        P.alias([bWU, bWD], bYT)
        upb = [(PS_A[:, 0, :], bPS_A[0]), (PS_A[:, 1, :], bPS_A[1]), (PS_B[:, :], bPS_Bq), (PS_C[:, :], bPS_Cr)]
        dnb = [(PS_D[:, :], bPS_Ds), (PS_E[:, :], bPS_E), (PS_F[:, :], bPS_F)]
        ui = 0
        di = 0
        for blk in range(4 if _STAGE >= 6 else 0):
            P.dma("gpsimd", lambda h, blk=blk: h.dma_start(out=WU, in_=w_up[blk]), writes=[bWU])
            P.dma("gpsimd", lambda h, blk=blk: h.dma_start(out=WD, in_=w_dn[blk]), writes=[bWD])
            for tg in range(NT // 4):
                tok0 = tg * 512
                hts = [bHT[4 * tg + i] for i in range(4)]
                for m in range(8):
                    pt, pb = upb[ui % 4]; ui += 1
                    for kc in range(8):
                        P.op("tensor", lambda h, pt=pt, kc=kc, m=m, tok0=tok0: h.matmul(
                            pt, lhsT=WU[:, kc, m * 128:(m + 1) * 128], rhs=HT[:, kc, tok0:tok0 + 512],
                            start=(kc == 0), stop=(kc == 7)), reads=hts + [bWU], writes=[pb])
                    rl, brl = RL[:, ui % 2, :], bRL[ui % 2]
                    P.op("scalar", lambda h, pt=pt, rl=rl: h.activation(out=rl, in_=pt, func=AF.Relu),
                         reads=[pb], writes=[brl])
                    P.op("gpsimd", lambda h, m=m, rl=rl: h.tensor_tensor(out=AT[:, m, :], in0=rl, in1=rl, op=ALU.mult),
                         reads=[brl], writes=[bAT])
                for i in range(4):
                    t = 4 * tg + i
                    for n in range(2):
                        pt, pb = dnb[di % 3]; di += 1
                        for m in range(8):
                            P.op("tensor", lambda h, pt=pt, m=m, n=n, i=i: h.matmul(
                                pt, lhsT=AT[:, m, i * 128:(i + 1) * 128], rhs=WD[:, m, n * 512:(n + 1) * 512],
                                start=(m == 0), stop=(m == 7)), reads=[bAT, bWD], writes=[pb])
                        P.op("vector", lambda h, pt=pt, n=n, t=t: h.tensor_tensor(
                            out=X[:, t, n * 512:(n + 1) * 512], in0=X[:, t, n * 512:(n + 1) * 512], in1=pt, op=ALU.add),
                            reads=[pb, bX[t]], writes=[bX[t]])

    if do_A:
        load_bcast(GB[:, :], bGB, gA_mix)
        for t in range(NT):
            rmsnorm_to_HT(t)
        P.op("gpsimd", lambda h: h.memset(BTOT[:, :], 0.0), writes=[bBTOT])
        P.alias([bSOUT], [bSALL, bPM, bHALO, bWO, bAT])
        for hd in range(nA):
            P.op("gpsimd", lambda h, hd=hd: h.memset(SST[hd][:, :], 0.0), writes=[bSST[hd]])
            P.dma("gpsimd", lambda h, hd=hd: h.dma_start(out=WH[:, :, 0:256], in_=wA_fi[hd]), writes=[bWH])
            scan_head(hd, False, 0, OMLA, bOMLA)
            P.op("gpsimd", lambda h, hd=hd: h.tensor_copy(out=SOUT[:, hd, 0:128], in_=SST[hd][:, :]),
                 reads=[bSST[hd]], writes=[bSOUT])
        P.op("scalar", lambda h: h.activation(out=SOUT[:, :, 128], in_=BTOT[:, :], func=AF.Exp),
             reads=[bBTOT], writes=[bSOUT])
        bSO = P.buf("sloc_o")
        P.dma("sync", lambda h: h.dma_start(out=sloc_out.rearrange("h p f -> p h f"), in_=SOUT),
              reads=[bSOUT], sembuf=bSO)
        store_bufs.append(bSO)
        P.dma("gpsimd", lambda h: h.dma_start(out=WP[:, :, :], in_=wA_pool[:, :, :]), writes=[bWP])
        tl = NT - 1
        for kc in range(8):
            P.op("tensor", lambda h, kc=kc, t=tl: h.matmul(PS_F[:, 0:256], lhsT=HT[:, kc, t * 128:(t + 1) * 128],
                                                     rhs=WP[:, kc, :], start=(kc == 0), stop=(kc == 7)),
                 reads=[bHT[tl], bWP], writes=[bPS_F])
        P.op("scalar", lambda h: h.activation(out=UP[:, 0, :], in_=PS_F[:, 0:256], func=AF.Copy),
             reads=[bPS_F], writes=[bUP[0]])
        bHO = P.buf("halo_o")
        P.dma("sync", lambda h: h.dma_start(out=halo_out[:, :], in_=UP[:, 0, :]), reads=[bUP[0]], sembuf=bHO)
        store_bufs.append(bHO)
        if do_B:
            bXO = P.buf("x_o")
            xo_v = x_out.rearrange("(t p) d -> p t d", p=128)
            for t in range(NT):
                P.dma("sync", lambda h, t=t: h.dma_start(out=xo_v[:, t, :], in_=X[:, t, :]), reads=[bX[t]], sembuf=bXO)
            store_bufs.append(bXO)

    if do_final:
        load_bcast(GB[:, :], bGB, g_fin)
        YF = v3(HT[:, :, :].rearrange("p a b -> p (a b)").bitcast(F32)[:, 0:2 * D], 2)
        bYF = [P.buf("YF0"), P.buf("YF1")]
        P.alias(bYF, bHT)
        bYO = P.buf("y_o")
        yo_v = y_out.rearrange("(t p) d -> p t d", p=128)
        for t in range(NT):
            xt = X[:, t, :]
            rms_stats(xt, bX[t])
            P.op("vector", lambda h, xt=xt, t=t: h.scalar_tensor_tensor(out=YF[:, t % 2, :], in0=xt, scalar=SS[:, 2:3], in1=GB[:, :],
                                                                       op0=ALU.mult, op1=ALU.mult),
                 reads=[bX[t], bSS, bGB], writes=[bYF[t % 2]])
            P.dma("sync", lambda h, t=t: h.dma_start(out=yo_v[:, t, :], in_=YF[:, t % 2, :]), reads=[bYF[t % 2]], sembuf=bYO)
        store_bufs.append(bYO)

    P.finish("sync", store_bufs)
    P.run()
    stack.close()
    return nc


def _consts():
    s = np.arange(128)
    ident = np.eye(128, dtype=np.float32)
    tri = (s[:, None] <= s[None, :]).astype(np.float32)
    tm = tri - tri[:, MID:MID + 1]
    tmx = np.concatenate([tm, tri[:, MID:MID + 1], np.ones((128, 1), np.float32)], axis=1)
    U = (s[:, None] > s[None, :]).astype(np.float32)
    con = np.concatenate([ident, tmx, U], axis=1).astype(np.float32)
    mask = (s[:, None] <= s[None, :]).astype(np.int32)
    return np.ascontiguousarray(con), np.ascontiguousarray(mask)


def _pool_mats(first_segment):
    out = np.zeros((128, 12, 128), np.float32)
    for g, w in enumerate(WINS):
        for t in range(128):
            for s_ in range(t - w + 1, t + 1):
                if s_ >= 0:
                    out[s_, 4 + g, t] += 1.0 / w
                else:
                    out[128 + s_, 8 + g, t] += 1.0 / w
            out[t, 4 + g, t] -= 1.0
            if first_segment:
                cnt = min(t + 1, w)
                for s_ in range(max(0, t - w + 1), t + 1):
                    out[s_, g, t] += 1.0 / cnt
                out[t, g, t] -= 1.0
        if not first_segment:
            out[:, g, :] = out[:, 4 + g, :]
    return out


_PROGS = {}
_STAGE = 99
_POOL = 9
_DBG = False


def _prog(key):
    if key not in _PROGS:
        _PROGS[key] = build_program(*key)
    return _PROGS[key]


def kernel(x, norm_mix_g, w_in, pool_w, pool_scale, hgrn_lb_logits, hgrn_norm_g,
           w_out, norm_mlp_g, w_up, w_down, final_norm_g):
    f32 = np.float32
    x = np.asarray(x, f32)
    w_in = np.asarray(w_in, f32)
    con, mask = _consts()
    lbl = np.ascontiguousarray(np.asarray(hgrn_lb_logits, f32).reshape(-1))
    xs = [np.ascontiguousarray(x[c // 4, (c % 4) * TOK:(c % 4 + 1) * TOK, :]) for c in range(NCORES)]
    pm = [_pool_mats(c % 4 == 0) for c in range(NCORES)]
    sels = []
    for c in range(NCORES):
        s_ = np.zeros((128, NCORES), f32)
        if c % 4 != 0:
            s_[:, c] = 1.0
        sels.append(s_)

    def lmask(l):
        m = np.zeros((128, DEPTH), f32)
        m[:, 1:l + 1] = 1.0
        return m

    def kc_layout(w):
        K, N = w.shape
        return np.ascontiguousarray(w.reshape(K // 128, 128, N).transpose(1, 0, 2))

    def layer_B(l):
        wi = w_in[l]
        heads = []
        for hd in range(NH):
            cols = [wi[:, 256 + j * 768 + hd * 128: 256 + j * 768 + (hd + 1) * 128] for j in range(4)]
            heads.append(kc_layout(np.concatenate(cols, axis=1)))
        pw2 = np.zeros((2, 128, 128), f32)
        for g in range(4):
            c2, o = g // 2, (g % 2) * 64
            pw2[c2, o:o + 64, o:o + 64] = pool_w[l, g]
        wu = np.asarray(w_up[l], f32)
        wd = np.asarray(w_down[l], f32)
        return {
            "lmB": lmask(l),
            "g_mix": np.ascontiguousarray(norm_mix_g[l], f32),
            "g_mlp": np.ascontiguousarray(norm_mlp_g[l], f32),
            "w_heads": np.ascontiguousarray(np.stack(heads, 0)),
            "w_pool": kc_layout(wi[:, 0:256]),
            "pool_w2": pw2,
            "pool_sc": np.ascontiguousarray(np.asarray(pool_scale[l], f32).reshape(2, 128).T),
            "ng": np.ascontiguousarray(hgrn_norm_g[l], f32),
            "w_out": kc_layout(np.asarray(w_out[l], f32)),
            "w_up": np.ascontiguousarray(np.stack([kc_layout(wu[:, b * 1024:(b + 1) * 1024]) for b in range(4)], 0)),
            "w_dn": np.ascontiguousarray(np.stack([kc_layout(wd[b * 1024:(b + 1) * 1024, :]) for b in range(4)], 0)),
        }

    def layer_A(l):
        wi = w_in[l]
        fi = []
        for hd in range(NH):
            cols = [wi[:, 256 + j * 768 + hd * 128: 256 + j * 768 + (hd + 1) * 128] for j in (1, 2)]
            fi.append(kc_layout(np.concatenate(cols, axis=1)))
        return {
            "lmA": lmask(l),
            "gA_mix": np.ascontiguousarray(norm_mix_g[l], f32),
            "wA_fi": np.ascontiguousarray(np.stack(fi, 0)),
            "wA_pool": kc_layout(wi[:, 0:256]),
        }

    common = {"consts": con, "maskd": mask, "lbl": lbl}
    cores = list(range(NCORES))

    la = layer_A(0)
    res = run_bass_kernel_spmd(_prog((False, True, False)),
                               [dict(common, x_in=xs[c], **la) for c in cores], core_ids=cores).results
    for l in range(DEPTH):
        sall = np.ascontiguousarray(np.stack([res[c]["sloc"] for c in cores], 0))
        halos = [np.zeros((128, 256), f32) if c % 4 == 0 else np.ascontiguousarray(res[c - 1]["halo_o"]) for c in cores]
        lb_ = layer_B(l)
        la = layer_A(min(l + 1, DEPTH - 1))
        maps = [dict(common, x_in=xs[c], sall=sall, halo=halos[c], sel=sels[c], pmat=pm[c], **lb_, **la) for c in cores]
        key = (True, True, False, 1) if l == DEPTH - 1 else (True, True, False)
        res = run_bass_kernel_spmd(_prog(key), maps, core_ids=cores).results
        xs = [np.ascontiguousarray(res[c]["x_out"]) for c in cores]
    gf = np.ascontiguousarray(final_norm_g, f32)
    res = run_bass_kernel_spmd(_prog((False, False, True)),
                               [dict(common, x_in=xs[c], g_fin=gf) for c in cores], core_ids=cores).results
    out = np.zeros((BATCH, SEQ, D), f32)
    for c in cores:
        out[c // 4, (c % 4) * TOK:(c % 4 + 1) * TOK, :] = res[c]["y_out"]
    return out
```
